# Optimizing a Trainium2 kernel written in Bass

```python
import math
import jax
import jax.numpy as jnp
from jax import lax
import numpy as np

D_MODEL = 1024
BATCH = 4
SEQ = 8192
DEPTH = 1

MIX_WIDTH = D_MODEL
DIFF_QK_DIM = 64
DIFF_V_DIM = 2 * DIFF_QK_DIM
WIDTH_DIFF = MIX_WIDTH // 2
N_HEADS_DIFF = WIDTH_DIFF // DIFF_V_DIM
HGRN_DIM = 128
WIDTH_HGRN = MIX_WIDTH - WIDTH_DIFF
N_HEADS_HGRN = WIDTH_HGRN // HGRN_DIM
IN_COLS = 3 * WIDTH_DIFF + 5 * WIDTH_HGRN
REL_BUCKETS = 32
REL_MAX_DIST = 128
Q_BLOCK = 128
CHUNK = 64
D_FF = 4 * D_MODEL
CONV_WIDTH = 3
PLE_DIM = 256
EPS = 1e-6

kernel_name = 'hybrid_diffattn_hgrn2_encoder_layer'


def rms_norm(x, g):
    xf = x.astype(jnp.float32)
    y = xf * lax.rsqrt(jnp.mean(xf * xf, axis=-1, keepdims=True) + EPS)
    return (y * g.astype(jnp.float32)).astype(x.dtype)


def t5_bucket(rel):
    half = REL_BUCKETS // 2
    max_exact = half // 2
    n = jnp.abs(rel)
    scaled = jnp.log(jnp.maximum(n, 1).astype(jnp.float32) / max_exact) / math.log(REL_MAX_DIST / max_exact)
    large = jnp.minimum(max_exact + (scaled * (half - max_exact)).astype(jnp.int32), half - 1)
    return jnp.where(rel > 0, half, 0) + jnp.where(n < max_exact, n, large)


def diff_attention(q, k, v, rel_bias, lam):
    b, t = q.shape[0], q.shape[1]
    nblk = t // Q_BLOCK
    q = q * (DIFF_QK_DIM ** -0.5)
    qb = jnp.moveaxis(q.reshape(b, nblk, Q_BLOCK, *q.shape[2:]), 1, 0)
    starts = jnp.arange(nblk, dtype=jnp.int32) * Q_BLOCK
    kpos = jnp.arange(t, dtype=jnp.int32)

    def block(args):
        qblk, start = args
        qpos = start + jnp.arange(Q_BLOCK, dtype=jnp.int32)
        bias = jnp.moveaxis(rel_bias[t5_bucket(kpos[None, :] - qpos[:, None])], -1, 0).astype(jnp.float32)
        logits = jnp.einsum('bqhcd,bkhcd->bhcqk', qblk, k, preferred_element_type=jnp.float32) + bias[None, :, None]
        probs = jax.nn.softmax(logits, axis=-1)
        attn = probs[:, :, 0] - lam * probs[:, :, 1]
        return jnp.einsum('bhqk,bkhd->bqhd', attn.astype(v.dtype), v)

    out = lax.map(block, (qb, starts))
    return jnp.moveaxis(out, 0, 1).reshape(b, t, *v.shape[2:])


def hgrn2_bidirectional(q, i, f_fwd, f_bwd, lb):
    b, t, h, d = q.shape
    nc = t // CHUNK
    rev = lambda a: jnp.flip(a, axis=1)
    qf = jax.nn.silu(q.astype(jnp.float32))
    qs = jnp.stack([qf, rev(qf)])
    vs = jnp.stack([i, rev(i)]).astype(jnp.float32)
    lbb = lb.astype(jnp.float32)[:, None, None]
    f = lbb + (1.0 - lbb) * jax.nn.sigmoid(jnp.stack([f_fwd, rev(f_bwd)]).astype(jnp.float32))
    ks = 1.0 - f
    logf = jnp.log(f)

    def to_chunks(a):
        return a.reshape(2, b, nc, CHUNK, h, d).transpose(2, 0, 1, 4, 3, 5)

    tri = jnp.tril(jnp.ones((CHUNK, CHUNK), dtype=bool))[:, :, None]

    def step(S, xs):
        qc, kc, lfc, vc = xs
        cum = jnp.cumsum(lfc, axis=-2)
        inter = jnp.einsum('nbhtk,nbhkv->nbhtv', qc * jnp.exp(cum), S)
        rel = jnp.where(tri, cum[..., :, None, :] - cum[..., None, :, :], -jnp.inf)
        scores = jnp.einsum('nbhtk,nbhsk,nbhtsk->nbhts', qc, kc, jnp.exp(rel))
        intra = jnp.einsum('nbhts,nbhsv->nbhtv', scores, vc)
        last = cum[..., -1:, :]
        S = jnp.exp(last[..., 0, :])[..., None] * S + jnp.einsum('nbhsk,nbhsv->nbhkv', kc * jnp.exp(last - cum), vc)
        return S, inter + intra

    S0 = jnp.zeros((2, b, h, d, d), jnp.float32)
    _, o = lax.scan(step, S0, (to_chunks(qs), to_chunks(ks), to_chunks(logf), to_chunks(vs)))
    o = o.transpose(1, 2, 0, 4, 3, 5).reshape(2, b, t, h, d)
    return o[0] + rev(o[1])


def conv_ffn(u, w_up, conv_w, conv_b, w_down):
    a = u @ w_up
    a = lax.conv_general_dilated(a, conv_w[:, None, :].astype(a.dtype), window_strides=(1,),
                                 padding=((CONV_WIDTH // 2, CONV_WIDTH // 2),),
                                 dimension_numbers=('NWC', 'WIO', 'NWC'),
                                 feature_group_count=a.shape[-1]) + conv_b.astype(a.dtype)
    gate, up = jnp.split(a, 2, axis=-1)
    return (jax.nn.gelu(gate, approximate=True) * up) @ w_down


def setup_inputs(seed: int = 0) -> dict:
    key = jax.random.key(seed)
    ks = jax.random.split(key, 24)
    f32 = jnp.float32

    def nrm(k, shape, scale):
        return jax.random.normal(k, shape, f32) * scale

    def gain(k, shape):
        return 1.0 + 0.05 * jax.random.normal(k, shape, f32)

    return {
        'x': nrm(ks[0], (BATCH, SEQ, D_MODEL), 1.0),
        'p': nrm(ks[1], (DEPTH, BATCH, SEQ, PLE_DIM), 1.0),
        'rel_bias': nrm(ks[2], (REL_BUCKETS, N_HEADS_DIFF), 0.5),
        'g_pre_mix': gain(ks[3], (DEPTH, D_MODEL)),
        'w_in': nrm(ks[4], (DEPTH, D_MODEL, IN_COLS), D_MODEL ** -0.5),
        'lambda_q1': nrm(ks[5], (DEPTH, DIFF_QK_DIM), 0.1),
        'lambda_k1': nrm(ks[6], (DEPTH, DIFF_QK_DIM), 0.1),
        'lambda_q2': nrm(ks[7], (DEPTH, DIFF_QK_DIM), 0.1),
        'lambda_k2': nrm(ks[8], (DEPTH, DIFF_QK_DIM), 0.1),
        'g_diff': gain(ks[9], (DEPTH, DIFF_V_DIM)),
        'lb_param': nrm(ks[10], (2, DEPTH + 1, WIDTH_HGRN), 1.0),
        'g_hgrn': gain(ks[11], (DEPTH, HGRN_DIM)),
        'w_out': nrm(ks[12], (DEPTH, MIX_WIDTH, D_MODEL), MIX_WIDTH ** -0.5),
        'g_post_mix': gain(ks[13], (DEPTH, D_MODEL)),
        'g_pre_ffn': gain(ks[14], (DEPTH, D_MODEL)),
        'w_up': nrm(ks[15], (DEPTH, D_MODEL, 2 * D_FF), D_MODEL ** -0.5),
        'conv_w': nrm(ks[16], (DEPTH, CONV_WIDTH, 2 * D_FF), CONV_WIDTH ** -0.5),
        'conv_b': nrm(ks[17], (DEPTH, 2 * D_FF), 0.01),
        'w_down': nrm(ks[18], (DEPTH, D_FF, D_MODEL), D_FF ** -0.5),
        'g_post_ffn': gain(ks[19], (DEPTH, D_MODEL)),
        'w_ple': nrm(ks[20], (DEPTH, PLE_DIM, D_MODEL), PLE_DIM ** -0.5),
        'g_ple': gain(ks[21], (DEPTH, D_MODEL)),
        'w_ple_gate': nrm(ks[22], (DEPTH, D_MODEL, D_MODEL), D_MODEL ** -0.5),
    }


def reference(x, p, rel_bias, g_pre_mix, w_in, lambda_q1, lambda_k1, lambda_q2, lambda_k2, g_diff,
              lb_param, g_hgrn, w_out, g_post_mix, g_pre_ffn, w_up, conv_w, conv_b, w_down,
              g_post_ffn, w_ple, g_ple, w_ple_gate):
    f32 = jnp.float32
    b, t, _ = x.shape
    wd, wh = WIDTH_DIFF, WIDTH_HGRN
    splits = [wd, 2 * wd, 3 * wd, 3 * wd + wh, 3 * wd + 2 * wh, 3 * wd + 3 * wh, 3 * wd + 4 * wh]
    lower_bounds = jnp.cumsum(jax.nn.softmax(lb_param.astype(f32), axis=1), axis=1)
    h = x
    for l in range(DEPTH):
        u = rms_norm(h, g_pre_mix[l])
        proj = u @ w_in[l]
        q_d, k_d, v_d, q_h, i_h, f_fw, f_bw, g_h = jnp.split(proj, splits, axis=-1)

        lam_init = 0.8 - 0.6 * math.exp(-0.3 * l)
        lam = (jnp.exp(jnp.sum(lambda_q1[l].astype(f32) * lambda_k1[l].astype(f32)))
               - jnp.exp(jnp.sum(lambda_q2[l].astype(f32) * lambda_k2[l].astype(f32))) + lam_init)
        qk_shape = (b, t, N_HEADS_DIFF, 2, DIFF_QK_DIM)
        a = diff_attention(q_d.reshape(qk_shape), k_d.reshape(qk_shape),
                           v_d.reshape(b, t, N_HEADS_DIFF, DIFF_V_DIM), rel_bias, lam)
        a = (rms_norm(a, g_diff[l]) * (1.0 - lam_init)).astype(x.dtype).reshape(b, t, WIDTH_DIFF)

        hd = (b, t, N_HEADS_HGRN, HGRN_DIM)
        lb = lower_bounds[:, l].reshape(2, N_HEADS_HGRN, HGRN_DIM)
        o = hgrn2_bidirectional(q_h.reshape(hd), i_h.reshape(hd), f_fw.reshape(hd), f_bw.reshape(hd), lb)
        o = (rms_norm(o, g_hgrn[l]) * jax.nn.silu(g_h.reshape(hd).astype(f32))).astype(x.dtype).reshape(b, t, WIDTH_HGRN)

        m = jnp.concatenate([a, o], axis=-1) @ w_out[l]
        h = h + rms_norm(m, g_post_mix[l])

        y = conv_ffn(rms_norm(h, g_pre_ffn[l]), w_up[l], conv_w[l], conv_b[l], w_down[l])
        h = h + rms_norm(y, g_post_ffn[l])

        e = rms_norm(p[l] @ w_ple[l], g_ple[l]) * jax.nn.sigmoid(h @ w_ple_gate[l])
        h = h + e
    return h
```

```python
import math
import numpy as np
import concourse.bass as bass
import concourse.mybir as mybir
from concourse.bass_utils import run_bass_kernel_spmd
from contextlib import ExitStack

F32 = mybir.dt.float32
BF16 = mybir.dt.bfloat16
AF = mybir.ActivationFunctionType
ALU = mybir.AluOpType
AX = mybir.AxisListType

D = 1024
EPS = 1e-6
SAME_ENGINE_WAITS = True
_ENVCFG = {}
ARENA_F32 = 49000


class Buf:
    __slots__ = ("name", "w", "r")

    def __init__(self, name):
        self.name = name
        self.w = {}
        self.r = {}


class T:
    __slots__ = ("ap", "b")

    def __init__(self, ap, b):
        self.ap = ap
        self.b = b


def _bufs(lst):
    out = []
    for x in lst:
        if x is None:
            continue
        out.append(x.b if isinstance(x, T) else x)
    return out


class K:
    ENG = ("pe", "act", "dve", "pool", "sp")

    def __init__(self, nc, es):
        self.nc = nc
        self.es = es
        self.sem = {e: es.enter_context(nc.semaphore("cs_" + e)) for e in self.ENG}
        self.semobj = {("c", e): self.sem[e] for e in self.ENG}
        self.cnt = {e: 0 for e in self.ENG}
        self.seen = {e: {} for e in self.ENG}
        self.prog = {e: [] for e in self.ENG}
        self.dcnt = {}
        self.nsem = 0
        self.arena = es.enter_context(nc.sbuf_tensor("arena", [128, ARENA_F32], F32))
        self.ps = es.enter_context(nc.psum_tensor("ps", [128, 8, 512], F32))
        self.off = 0
        self.bank = [T(self.ps[:, i, :], Buf("bank%d" % i)) for i in range(8)]
        self.nb = 0

    def sb(self, shape, dt, name=None):
        n = 1
        for s in shape[1:]:
            n *= s
        words = (n * (2 if dt == BF16 else 4) + 3) // 4
        words = (words + 7) // 8 * 8
        assert self.off + words <= ARENA_F32, ("SBUF arena overflow", self.off, words, name)
        ap = self.arena[0:shape[0], self.off:self.off + words]
        self.off += words
        if dt == BF16:
            ap = ap.bitcast(BF16)[:, 0:n]
        else:
            ap = ap[:, 0:n]
        if len(shape) == 3:
            ap = ap.rearrange("p (a b) -> p a b", b=shape[2])
        elif len(shape) == 4:
            ap = ap.rearrange("p (a b c) -> p a b c", b=shape[2], c=shape[3])
        self.nb += 1
        return T(ap, Buf(name or ("t%d" % self.nb)))

    def bank_bf(self, i, a=8, b=128):
        return self.ps[:, i, :].bitcast(BF16).rearrange("p (a b) -> p a b", b=b)

    def _waits(self, e, deps):
        own = ("c", e)
        for key, val in deps.items():
            if key == own:
                if e == "pe" or e == "sp" or not SAME_ENGINE_WAITS or val > self.cnt[e]:
                    continue
            if self.seen[e].get(key, 0) >= val:
                continue
            self.seen[e][key] = val
            so = self.semobj[key]
            self.prog[e].append(("w", so, val))

    @staticmethod
    def _merge(d, src):
        for k_, v in src.items():
            if d.get(k_, 0) < v:
                d[k_] = v

    def op(self, e, emit, reads=(), writes=(), inc=True):
        R = _bufs(reads)
        W = _bufs(writes)
        deps = {}
        for b in R:
            self._merge(deps, b.w)
        for b in W:
            self._merge(deps, b.w)
            self._merge(deps, b.r)
        self._waits(e, deps)
        val = self.cnt[e] + 1
        if inc:
            self.cnt[e] = val
        self.prog[e].append(("i", emit, self.sem[e] if inc else None, 1))
        key = ("c", e)
        for b in R:
            if b.r.get(key, 0) < val:
                b.r[key] = val
        for b in W:
            b.w = {key: val}
            b.r = {}

    def dma(self, out, in_, reads=(), writes=(), slot=None, q="sp", slow=False):
        R = _bufs(reads)
        W = _bufs(writes)
        sl = slot.b if isinstance(slot, T) else slot
        deps = {}
        for b in R:
            self._merge(deps, b.w)
        for b in W:
            self._merge(deps, b.w)
            self._merge(deps, b.r)
        self._waits(q, deps)
        key = ("d", id(sl))
        if key not in self.semobj:
            self.nsem += 1
            self.semobj[key] = self.es.enter_context(self.nc.semaphore("ds%d" % self.nsem))
            self.dcnt[key] = 0
        self.dcnt[key] += 16
        val = self.dcnt[key]
        so = self.semobj[key]
        if slow:
            self.prog[q].append(("i", lambda eng: eng.dma_start(out=out, in_=in_, allow_slow_non_contiguous=True), so, 16))
        else:
            self.prog[q].append(("i", lambda eng: eng.dma_start(out=out, in_=in_), so, 16))
        for b in R:
            if b.r.get(key, 0) < val:
                b.r[key] = val
        for b in W:
            if b is sl:
                b.w = {key: val}
                b.r = {}
            else:
                b.w[key] = val

    def barrier(self, engines=None):
        allv = {}
        for e in self.ENG:
            if self.cnt[e] > 0:
                allv[("c", e)] = self.cnt[e]
        for key, v in self.dcnt.items():
            if v > 0:
                allv[key] = v
        for e in (engines or self.ENG):
            deps = {k_: v for k_, v in allv.items() if k_ != ("c", e)}
            self._waits(e, deps)

    def emit_all(self):
        nc = self.nc
        block = self.es.enter_context(nc.Block())
        prog = self.prog

        def run(eng, lst):
            for it in lst:
                if it[0] == "w":
                    eng.wait_ge(it[1], it[2])
                else:
                    ins = it[1](eng)
                    if it[2] is not None:
                        ins.then_inc(it[2], it[3])

        @block.sync
        def _(eng):
            run(eng, prog["sp"])

        @block.scalar
        def _(eng):
            run(eng, prog["act"])

        @block.vector
        def _(eng):
            run(eng, prog["dve"])

        @block.gpsimd
        def _(eng):
            run(eng, prog["pool"])

        @block.tensor
        def _(eng):
            run(eng, prog["pe"])

    def act(self, out, in_, func, R, W, **kw):
        self.op("act", lambda e: e.activation(out=out, in_=in_, func=func, **kw), R, W)

    def tt(self, eng, out, in0, in1, op, R, W):
        self.op(eng, lambda e: e.tensor_tensor(out=out, in0=in0, in1=in1, op=op), R, W)

    def ts(self, eng, out, in0, s1, s2, op0, op1, R, W):
        if op1 is None:
            self.op(eng, lambda e: e.tensor_scalar(out=out, in0=in0, scalar1=s1, scalar2=None, op0=op0), R, W)
        else:
            self.op(eng, lambda e: e.tensor_scalar(out=out, in0=in0, scalar1=s1, scalar2=s2, op0=op0, op1=op1), R, W)

    def stt(self, out, in0, scalar, in1, op0, op1, R, W):
        self.op("dve", lambda e: e.scalar_tensor_tensor(out=out, in0=in0, scalar=scalar, in1=in1, op0=op0, op1=op1), R, W)

    def copy(self, eng, out, in_, R, W):
        if eng == "act":
            self.op("act", lambda e: e.activation(out=out, in_=in_, func=AF.Copy), R, W)
        else:
            self.op(eng, lambda e: e.tensor_copy(out=out, in_=in_), R, W)

    def recip(self, out, in_, R, W):
        self.op("dve", lambda e: e.reciprocal(out=out, in_=in_), R, W)

    def memset(self, eng, out, val, W):
        self.op(eng, lambda e: e.memset(out, val), [], W)

    def mm(self, out, lhsT, rhs, start, stop, R, W, inc=True):
        self.op("pe", lambda e: e.matmul(out, lhsT=lhsT, rhs=rhs, start=start, stop=stop), R, W, inc=inc)

    def tr(self, out, in_, ident, R, W, inc=True):
        self.op("pe", lambda e: e.transpose(out=out, in_=in_, identity=ident), R, W, inc=inc)

    def rstd(self, ss, tmp, out, n, R, W):
        self.act(tmp.ap, ss, AF.Ln, list(R) + [self.eps], [tmp], scale=1.0 / n, bias=self.eps.ap)
        self.act(out, tmp.ap, AF.Exp, [tmp], W, scale=-0.5)


def build(NB, NO, dbg=(), NPH=7):
    NQ = NO + 1
    TT, TQ, TO = NB * 128, NQ * 128, NO * 128
    nc = bass.Bass("TRN2", target_bir_lowering=False)

    def din(name, shape, dt=F32):
        return nc.dram_tensor(name, list(shape), dt, kind="ExternalInput").ap()

    def dscr(name, shape, dt):
        kind = "ExternalOutput" if name in dbg else "Internal"
        return T(nc.dram_tensor(name, list(shape), dt, kind=kind).ap(), Buf(name))

    x = din("x", [TT, D])
    p_in = din("p", [TO, 256])
    w_in = din("w_in", [D, 4096])
    w_out = din("w_out", [D, D])
    w_up = din("w_up", [D, 8192])
    w_down = din("w_down", [4096, D])
    w_ple = din("w_ple", [256, D])
    w_gate = din("w_gate", [D, D])
    gcols = din("gcols", [128, 16])
    grow = din("grow", [128, 3328])
    lbp = din("lbp", [128, 2048])
    lamv = din("lamv", [128, 256])
    convc = din("convc", [128, 256])
    bstrip = din("bstrip", [4, 128, 9 * 128])
    cst = din("cst", [128, 904])
    out_d = nc.dram_tensor("out", [TO, D], F32, kind="ExternalOutput").ap()

    WIN = dscr("WIN", [D, 4096], BF16)
    WOUT = dscr("WOUT", [D, D], BF16)
    WUPS = dscr("WUPS", [32, 128, 2, 8, 128], BF16)
    WDN = dscr("WDN", [4096, D], BF16)
    WPLE = dscr("WPLE", [256, D], BF16)
    WG = dscr("WG", [D, D], BF16)
    QT = dscr("QT", [4, 128, TQ], BF16)
    KT = dscr("KT", [4, 128, TT], BF16)
    VA = dscr("VA", [TT, 4, 132], BF16)
    HQ = dscr("HQ", [TQ, 512], BF16)
    HI = dscr("HI", [TT, 512], BF16)
    HG = dscr("HG", [TQ, 512], BF16)
    LFF = dscr("LFF", [TQ, 512], F32)
    LFB = dscr("LFB", [TT, 512], F32)
    OB = dscr("OB", [TQ, 512], F32)
    CAT = dscr("CAT", [TQ, D], BF16)
    H2 = dscr("H2", [TO, D], F32)
    U2T = dscr("U2T", [8, 128, TO + 2], BF16)
    H3 = dscr("H3", [TO, D], F32)

    es = ExitStack()
    with es:
        k = K(nc, es)
        bank = k.bank
        cst_t = k.sb([128, 904], F32, "cst")
        grow_t = k.sb([128, 3328], F32, "grow")
        ident = k.sb([128, 128], BF16, "ident")
        k.eps = k.sb([128, 1], F32, "eps")
        k.dma(cst_t.ap, cst, [], [cst_t], cst_t)
        k.dma(grow_t.ap, grow, [], [grow_t], grow_t)
        k.copy("dve", ident.ap, cst_t.ap[:, 0:128], [cst_t], [ident])
        k.memset("pool", k.eps.ap, EPS, [k.eps])
        Uf = cst_t.ap[:, 128:256]
        Ub = cst_t.ap[:, 256:384]
        JUf = cst_t.ap[:, 384:512]
        JUb = cst_t.ap[:, 512:640]
        Jsel = cst_t.ap[:, 896:898]
        base_off = k.off

        def new_phase():
            k.barrier()
            k.off = base_off

        def phase_p0():
            gc = k.sb([128, 16], F32, "gcols")
            k.dma(gc.ap, gcols, [], [gc], gc)
            stf = [k.sb([128, 2048], F32, "stf%d" % i) for i in range(4)]
            stb = [k.sb([128, 2048], BF16, "stb%d" % i) for i in range(4)]
            jobs = []
            for kc in range(8):
                for cp in range(2):
                    jobs.append((w_in[kc * 128:(kc + 1) * 128, cp * 2048:(cp + 1) * 2048], 2048,
                                 WIN, WIN.ap[kc * 128:(kc + 1) * 128, cp * 2048:(cp + 1) * 2048], None, gc.ap[:, kc:kc + 1]))
            for kc in range(8):
                jobs.append((w_out[kc * 128:(kc + 1) * 128, :], 1024, WOUT, WOUT.ap[kc * 128:(kc + 1) * 128, :], None, None))
            for kc in range(8):
                for cp in range(4):
                    g = cp // 2
                    fc0 = (cp % 2) * 16
                    dst = WUPS.ap[fc0:fc0 + 16, :, g, kc, :].rearrange("fc p n -> p fc n")
                    jobs.append((w_up[kc * 128:(kc + 1) * 128, cp * 2048:(cp + 1) * 2048], 2048, WUPS, dst, 128, gc.ap[:, 8 + kc:9 + kc]))
            for rc in range(32):
                jobs.append((w_down[rc * 128:(rc + 1) * 128, :], 1024, WDN, WDN.ap[rc * 128:(rc + 1) * 128, :], None, None))
            for rc in range(2):
                jobs.append((w_ple[rc * 128:(rc + 1) * 128, :], 1024, WPLE, WPLE.ap[rc * 128:(rc + 1) * 128, :], None, None))
            for kc in range(8):
                jobs.append((w_gate[kc * 128:(kc + 1) * 128, :], 1024, WG, WG.ap[kc * 128:(kc + 1) * 128, :], None, None))
            def run_job(i):
                src, w, dbuf, dst, split, sc = jobs[i]
                sf = stf[i % 4]
                sbt = stb[i % 4]
                dq = "pool"
                k.dma(sf.ap[:, 0:w], src, [], [sf], sf, q=dq)
                if i % 2 == 0:
                    if sc is not None:
                        k.ts("dve", sbt.ap[:, 0:w], sf.ap[:, 0:w], sc, None, ALU.mult, None, [sf, gc], [sbt])
                    else:
                        k.copy("dve", sbt.ap[:, 0:w], sf.ap[:, 0:w], [sf], [sbt])
                else:
                    if sc is not None:
                        k.act(sbt.ap[:, 0:w], sf.ap[:, 0:w], AF.Copy, [sf, gc], [sbt], scale=sc)
                    else:
                        k.copy("act", sbt.ap[:, 0:w], sf.ap[:, 0:w], [sf], [sbt])
                srcap = sbt.ap[:, 0:w]
                if split:
                    srcap = srcap.rearrange("p (a b) -> p a b", b=split)
                k.dma(dst, srcap, [sbt], [dbuf], sbt, q=dq)

            for i in range(16):
                run_job(i)
            pending = list(range(16, len(jobs)))

            def more_jobs(n):
                for _ in range(n):
                    if pending:
                        run_job(pending.pop(0))
            return more_jobs

        def phase_pa(more_jobs):
            win = k.sb([128, 8, 4096], BF16, "win")
            for kc in range(8):
                k.dma(win.ap[:, kc, :], WIN.ap[kc * 128:(kc + 1) * 128, :], [WIN], [win], win)
            lbr = k.sb([128, 2, 2, 512], F32, "lbr")
            k.dma(lbr.ap, lbp.rearrange("p (a b c) -> p a b c", a=2, b=2), [], [lbr], lbr)
            lbt = k.sb([128, 2, 512], F32, "lbt")
            k.tt("dve", lbt.ap, lbr.ap[:, :, 1, :], lbr.ap[:, :, 0, :], ALU.subtract, [lbr], [lbt])
            k.act(lbt.ap, lbt.ap, AF.Exp, [lbt], [lbt])
            k.ts("dve", lbt.ap, lbt.ap, 1.0, None, ALU.add, None, [lbt], [lbt])
            k.recip(lbt.ap, lbt.ap, [lbt], [lbt])
            xt = [k.sb([128, 1024], F32, "xt%d" % i) for i in range(2)]
            junk = k.sb([128, 1024], BF16, "junk")
            ss = k.sb([128, 1], F32, "ss")
            lnv = k.sb([128, 1], F32, "lnv")
            rs = k.sb([128, 1], F32, "rs")
            xb = k.sb([128, 1024], BF16, "xb")
            uT = [k.sb([128, 8, 128], BF16, "uT%d" % i) for i in range(2)]
            qd = [k.sb([128, 512], BF16, "qd%d" % i) for i in range(2)]
            tq = [k.sb([128, 4, 128], BF16, "tq%d" % i) for i in range(4)]
            vas = [k.sb([128, 4, 132], BF16, "vas%d" % i) for i in range(2)]
            for v in vas:
                k.memset("pool", v.ap[:, :, 128:132], 1.0, [v])
            hst = [k.sb([128, 512], BF16, "hst%d" % i) for i in range(4)]
            lfs = [k.sb([128, 512], F32, "lfs%d" % i) for i in range(3)]
            tmp = [k.sb([128, 512], F32, "ptmp%d" % i) for i in range(6)]
            cnt = {"q": 0, "tq": 0, "va": 0, "h": 0, "lf": 0, "t": 0, "pj": 0}
            bT = T(k.bank_bf(7), bank[7].b)
            bQ = [T(k.bank_bf(5)[:, 0:4, :], bank[5].b), T(k.bank_bf(6)[:, 0:4, :], bank[6].b)]

            def nxt(name, lst):
                i = cnt[name]
                cnt[name] += 1
                return lst[i % len(lst)]

            def load_x(tb):
                k.dma(xt[tb % 2].ap, x[tb * 128:(tb + 1) * 128, :], [], [xt[tb % 2]], xt[tb % 2])

            def norm_t(tb):
                xx = xt[tb % 2]
                u = uT[tb % 2]
                k.act(junk.ap, xx.ap, AF.Square, [xx], [junk, ss], accum_out=ss.ap)
                k.rstd(ss.ap, lnv, rs.ap, 1024, [ss], [rs])
                k.ts("dve", xb.ap, xx.ap, rs.ap, None, ALU.mult, None, [xx, rs], [xb])
                for kc in range(8):
                    k.tr(bT.ap[:, kc, :], xb.ap[:, kc * 128:(kc + 1) * 128], ident.ap, [xb, ident], [bT], inc=(kc == 7))
                k.copy("act", u.ap, bT.ap, [bT], [u])

            load_x(0)
            if NB > 1:
                load_x(1)
            norm_t(0)
            for tb in range(NB):
                own = tb < NQ
                if tb + 2 < NB:
                    load_x(tb + 2)
                if tb + 1 < NB:
                    norm_t(tb + 1)
                u = uT[tb % 2]
                rows = slice(tb * 128, (tb + 1) * 128)
                pairs = [(3, 7), (5, 6), (0, 1), (2, 4)] if own else [(6, 1), (2, 4)]

                def stages_for(g, pj):
                    st = []
                    if g in (0, 1):
                        q_ = nxt("q", qd)
                        bq = bQ[g]
                        t_ = nxt("tq", tq)
                        dst = QT if g == 0 else KT
                        st.append(lambda: k.copy("dve", q_.ap, pj.ap, [pj], [q_]))

                        def trs():
                            for h in range(4):
                                k.tr(bq.ap[:, h, :], q_.ap[:, h * 128:(h + 1) * 128], ident.ap, [q_, ident], [bq], inc=(h == 3))
                        st.append(trs)
                        st.append(lambda: k.copy("act", t_.ap, bq.ap, [bq], [t_]))
                        st.append(lambda: k.dma(dst.ap[:, :, tb * 128:(tb + 1) * 128].rearrange("h p t -> p h t"), t_.ap, [t_], [dst], t_))
                    elif g == 2:
                        v_ = nxt("va", vas)
                        st.append(lambda: k.copy("act", v_.ap[:, :, 0:128], pj.ap.rearrange("p (h c) -> p h c", c=128), [pj], [v_]))
                        st.append(lambda: k.dma(VA.ap[rows, :, :], v_.ap, [v_], [VA], v_))
                    elif g in (3, 7):
                        e_ = nxt("t", tmp)
                        h_ = nxt("h", hst)
                        dst = HQ if g == 3 else HG
                        st.append(lambda: k.act(e_.ap, pj.ap, AF.Exp, [pj], [e_], scale=-1.0))
                        st.append(lambda: k.ts("dve", e_.ap, e_.ap, 1.0, None, ALU.add, None, [e_], [e_]))
                        st.append(lambda: k.recip(e_.ap, e_.ap, [e_], [e_]))
                        st.append(lambda: k.tt("dve", h_.ap, pj.ap, e_.ap, ALU.mult, [pj, e_], [h_]))
                        st.append(lambda: k.dma(dst.ap[rows, :], h_.ap, [h_], [dst], h_))
                    elif g == 4:
                        h_ = nxt("h", hst)
                        st.append(lambda: k.copy("dve", h_.ap, pj.ap, [pj], [h_]))
                        st.append(lambda: k.dma(HI.ap[rows, :], h_.ap, [h_], [HI], h_))
                    else:
                        d_ = g - 5
                        e_ = nxt("t", tmp)
                        a_ = nxt("t", tmp)
                        l_ = nxt("lf", lfs)
                        dst = LFF if g == 5 else LFB
                        st.append(lambda: k.act(e_.ap, pj.ap, AF.Exp, [pj], [e_], scale=-1.0))
                        st.append(lambda: k.tt("dve", a_.ap, e_.ap, lbt.ap[:, d_, :], ALU.mult, [e_, lbt], [a_]))
                        st.append(lambda: k.act(a_.ap, a_.ap, AF.Ln, [a_], [a_], bias=1.0))
                        st.append(lambda: k.act(e_.ap, e_.ap, AF.Ln, [e_], [e_], bias=1.0))
                        st.append(lambda: k.tt("dve", l_.ap, a_.ap, e_.ap, ALU.subtract, [a_, e_], [l_]))
                        st.append(lambda: k.dma(dst.ap[rows, :], l_.ap, [l_], [dst], l_))
                    return st

                for pair in pairs:
                    sts = []
                    for g in pair:
                        pj = bank[cnt["pj"] % 5]
                        cnt["pj"] += 1
                        for kc in range(8):
                            k.mm(pj.ap, u.ap[:, kc, :], win.ap[:, kc, g * 512:(g + 1) * 512], kc == 0, kc == 7,
                                 [u, win], [pj], inc=(kc == 7))
                        sts.append(stages_for(g, pj))
                    for i in range(max(len(x_) for x_ in sts)):
                        for st in sts:
                            if i < len(st):
                                st[i]()
                more_jobs(2)
            more_jobs(1000)

        def phase_att():
            bs = k.sb([128, 4, 9, 128], F32, "bs")
            k.dma(bs.ap, bstrip.rearrange("h p (d q) -> p h d q", q=128), [], [bs], bs)
            lv = k.sb([128, 256], F32, "lamv")
            k.dma(lv.ap, lamv, [], [lv], lv)
            pr = k.sb([128, 2, 64], F32, "lprod")
            k.tt("dve", pr.ap[:, 0, :], lv.ap[:, 0:64], lv.ap[:, 64:128], ALU.mult, [lv], [pr])
            k.tt("dve", pr.ap[:, 1, :], lv.ap[:, 128:192], lv.ap[:, 192:256], ALU.mult, [lv, pr], [pr])
            s12 = k.sb([128, 2], F32, "s12")
            k.op("dve", lambda e: e.tensor_reduce(out=s12.ap, in_=pr.ap, axis=AX.X, op=ALU.add), [pr], [s12])
            k.act(s12.ap, s12.ap, AF.Exp, [s12], [s12])
            nlam = k.sb([128, 1], F32, "nlam")
            k.tt("dve", nlam.ap, s12.ap[:, 0:1], s12.ap[:, 1:2], ALU.subtract, [s12], [nlam])
            k.ts("dve", nlam.ap, nlam.ap, -1.0, -0.2, ALU.mult, ALU.add, [nlam], [nlam])
            gd8 = k.sb([128, 128], F32, "gd8")
            k.ts("dve", gd8.ap, grow_t.ap[:, 3072:3200], 0.8, None, ALU.mult, None, [grow_t], [gd8])
            kt = [k.sb([128, TT], BF16, "kt%d" % i) for i in range(2)]
            va = [k.sb([128, NB, 132], BF16, "va%d" % i) for i in range(2)]
            qt = [[k.sb([128, TQ], BF16, "qt%d_%d" % (c, i)) for i in range(2)] for c in range(2)]
            for c in range(2):
                for i in range(2):
                    k.memset("dve", qt[c][i].ap, 0.0, [qt[c][i]])
            PT = [k.sb([128, 2, 512], BF16, "PT%d" % i) for i in range(3)]
            bs8 = k.sb([128, 4, 9, 128], BF16, "bs8")
            k.ts("dve", bs8.ap, bs.ap, 8.0, None, ALU.mult, None, [bs], [bs8])
            cst_ = [k.sb([128, 128], BF16, "cst%d" % i) for i in range(4)]
            a0 = [k.sb([128, 128], F32, "a0%d" % i) for i in range(2)]
            aj = k.sb([128, 128], BF16, "ajunk")
            rc = [k.sb([128, 2], F32, "rc%d" % i) for i in range(2)]
            ssq = [k.sb([128, 1], F32, "ssq%d" % i) for i in range(2)]
            lnq = [k.sb([128, 1], F32, "lnq%d" % i) for i in range(2)]
            rsq = [k.sb([128, 1], F32, "rsq%d" % i) for i in range(2)]

            def load_head(h):
                s = h % 2
                npc = 4 if TT >= 2048 else 1
                w = TT // npc
                for i in range(npc):
                    k.dma(kt[s].ap[:, i * w:(i + 1) * w], KT.ap[h, :, i * w:(i + 1) * w], [KT], [kt[s]], kt[s])
                k.dma(va[s].ap, VA.ap[:, h, :].rearrange("(j p) c -> p j c", p=128), [VA], [va[s]], va[s])
                k.dma(qt[0][s].ap[0:64, :], QT.ap[h, 0:64, :], [QT], [qt[0][s]], qt[0][s])
                k.dma(qt[1][s].ap[64:128, :], QT.ap[h, 64:128, :], [QT], [qt[1][s]], qt[1][s])

            groups = []
            i = 0
            while i < NQ:
                nq = min(2, NQ - i)
                groups.append((i, nq))
                i += nq
            NKP = NB // 2
            steps = [(h, gi, kp) for h in range(4) for gi in range(len(groups)) for kp in range(NKP)]
            ctr = {"pt": 0, "nb": 0, "cs": 0, "ep": 0}

            def S_bank(si, c):
                return bank[(si % 2) * 2 + c]

            def is_near(si):
                h, gi, kp = steps[si]
                i0, nq = groups[gi]
                return not (2 * kp + 1 <= i0 - 2 or 2 * kp >= i0 + nq - 1 + 2)

            def qk(si):
                h, gi, kp = steps[si]
                i0, nq = groups[gi]
                N = nq * 128
                s = h % 2
                near = is_near(si)
                for c in range(2):
                    S = S_bank(si, c)
                    for jj in range(2):
                        j = 2 * kp + jj
                        k.mm(S.ap[:, jj * N:(jj + 1) * N], kt[s].ap[:, j * 128:(j + 1) * 128],
                             qt[c][s].ap[:, i0 * 128:i0 * 128 + N], True, not near, [kt[s], qt[c][s]], [S], inc=(jj == 1 and not near))
                        if near:
                            for ii in range(nq):
                                dl = max(-4, min(4, j - (i0 + ii)))
                                sl = slice(jj * N + ii * 128, jj * N + (ii + 1) * 128)
                                k.mm(S.ap[:, sl], ident.ap, bs8.ap[:, h, dl + 4, :], False, ii == nq - 1, [ident, bs8], [S],
                                     inc=(jj == 1 and ii == nq - 1))

            def expv(si):
                h, gi, kp = steps[si]
                i0, nq = groups[gi]
                N = nq * 128
                j0, j1 = 2 * kp, 2 * kp + 1
                imin, imax = i0, i0 + nq - 1
                m = si % 2
                S0, S1 = bank[2 * m], bank[2 * m + 1]
                Sin = k.ps[:, 2 * m:2 * m + 2, 0:2 * N]
                P = PT[ctr["pt"] % len(PT)]
                ctr["pt"] += 1
                if is_near(si):
                    k.act(P.ap[:, :, 0:2 * N], Sin, AF.Exp, [S0, S1], [P], scale=0.125)
                elif j1 <= imin - 2:
                    k.act(P.ap[:, :, 0:2 * N], Sin, AF.Exp, [S0, S1, bs], [P], scale=0.125, bias=bs.ap[:, h, 0, 0:1])
                else:
                    k.act(P.ap[:, :, 0:2 * N], Sin, AF.Exp, [S0, S1, bs], [P], scale=0.125, bias=bs.ap[:, h, 8, 0:1])
                return P

            def pv(si, pts):
                h, gi, kp = steps[si]
                i0, nq = groups[gi]
                N = nq * 128
                s = h % 2
                P = pts
                for c in range(2):
                    for jj in range(2):
                        j = 2 * kp + jj
                        for ii in range(nq):
                            O = bank[4 + ii * 2 + c]
                            k.mm(O.ap[:, 0:129], P.ap[:, c, jj * N + ii * 128:jj * N + (ii + 1) * 128], va[s].ap[:, j, 0:129],
                                 j == 0, j == NB - 1, [P, va[s]], [O], inc=(jj == 1 and ii == nq - 1))

            def epilogue(si):
                h, gi, kp = steps[si]
                i0, nq = groups[gi]
                for ii in range(nq):
                    e = ctr["ep"] % 2
                    ctr["ep"] += 1
                    O0, O1 = bank[4 + ii * 2], bank[4 + ii * 2 + 1]
                    k.recip(rc[e].ap[:, 0:1], O0.ap[:, 128:129], [O0], [rc[e]])
                    k.recip(rc[e].ap[:, 1:2], O1.ap[:, 128:129], [O1, rc[e]], [rc[e]])
                    k.tt("dve", rc[e].ap[:, 1:2], rc[e].ap[:, 1:2], nlam.ap, ALU.mult, [rc[e], nlam], [rc[e]])
                    k.ts("dve", a0[e].ap, O0.ap[:, 0:128], rc[e].ap[:, 0:1], None, ALU.mult, None, [O0, rc[e]], [a0[e]])
                    k.stt(a0[e].ap, O1.ap[:, 0:128], rc[e].ap[:, 1:2], a0[e].ap, ALU.mult, ALU.add, [O1, rc[e], a0[e]], [a0[e]])
                    k.act(aj.ap, a0[e].ap, AF.Square, [a0[e]], [aj, ssq[e]], accum_out=ssq[e].ap)
                    k.rstd(ssq[e].ap, lnq[e], rsq[e].ap, 128, [ssq[e]], [rsq[e]])
                    cs = cst_[ctr["cs"] % 4]
                    ctr["cs"] += 1
                    k.stt(cs.ap, a0[e].ap, rsq[e].ap, gd8.ap, ALU.mult, ALU.mult, [a0[e], rsq[e], gd8], [cs])
                    r0 = (i0 + ii) * 128
                    k.dma(CAT.ap[r0:r0 + 128, h * 128:(h + 1) * 128], cs.ap, [cs], [CAT], cs)

            load_head(0)
            qk(0)
            for si in range(len(steps)):
                h, gi, kp = steps[si]
                if gi == 0 and kp == 0 and h + 1 < 4:
                    load_head(h + 1)
                if si + 1 < len(steps):
                    qk(si + 1)
                pts = expv(si)
                pv(si, pts)
                if kp == NKP - 1:
                    epilogue(si)

        def phase_hg():
            S32 = [k.sb([128, 128], F32, "S32_%d" % i) for i in range(8)]
            Sbf = [k.sb([128, 128], BF16, "Sbf_%d" % i) for i in range(8)]
            for i in range(8):
                k.memset("pool", S32[i].ap, 0.0, [S32[i]])
                k.memset("pool", Sbf[i].ap, 0.0, [Sbf[i]])
            lf = [k.sb([128, 512], F32, "lf%d" % i) for i in range(3)]
            hq = [k.sb([128, 512], BF16, "hq%d" % i) for i in range(3)]
            hi = [k.sb([128, 512], BF16, "hi%d" % i) for i in range(3)]
            hg = [k.sb([128, 512], BF16, "hg%d" % i) for i in range(3)]
            ob = [k.sb([128, 512], F32, "ob%d" % i) for i in range(3)]
            b_t = k.sb([128, 512], F32, "b_t")
            ib_t = k.sb([128, 512], F32, "ib_t")
            ed_t = k.sb([128, 512], F32, "ed_t")
            k_t = k.sb([128, 512], F32, "k_t")
            qk_ = k.sb([128, 1024], BF16, "qk")
            kha = [k.sb([128, 512], BF16, "kha%d" % i) for i in range(2)]
            khb = [k.sb([128, 512], BF16, "khb%d" % i) for i in range(2)]
            for i in range(2):
                k.memset("dve", kha[i].ap, 0.0, [kha[i]])
                k.memset("dve", khb[i].ap, 0.0, [khb[i]])
            dec = [k.sb([128, 8], F32, "dec%d" % i) for i in range(2)]
            qkT2 = [k.sb([128, 8, 128], BF16, "qkT%d" % i) for i in range(2)]
            qTa2 = [k.sb([128, 4, 128], BF16, "qTa%d" % i) for i in range(2)]
            qTb2 = [k.sb([128, 4, 128], BF16, "qTb%d" % i) for i in range(2)]
            for i in range(2):
                k.memset("dve", qTa2[i].ap, 0.0, [qTa2[i]])
                k.memset("dve", qTb2[i].ap, 0.0, [qTb2[i]])
            AT2 = [[k.sb([128, 128], BF16, "AT%d_%d" % (j, i)) for i in range(4)] for j in range(2)]
            osb = [k.sb([128, 512], F32, "osb%d" % i) for i in range(2)]
            sq = k.sb([128, 512], F32, "sq")
            ss4 = k.sb([128, 4], F32, "ss4")
            ln4 = k.sb([128, 4], F32, "ln4")
            rs4 = k.sb([128, 4], F32, "rs4")
            on = k.sb([128, 512], F32, "on")
            c2 = [k.sb([128, 512], BF16, "c2_%d" % i) for i in range(2)]
            gh = grow_t.ap[:, 3200:3328]
            bA, bB, bC, bO, bO2 = bank[0], bank[1], bank[2], bank[5], bank[7]
            bT = T(k.bank_bf(3), bank[3].b)
            bSC = [T(bank[4].ap[:, i * 128:(i + 1) * 128], bank[4].b) for i in range(4)]
            bSU = [T(bank[6].ap[:, i * 128:(i + 1) * 128], bank[6].b) for i in range(4)]
            bOh = [T(bO.ap[:, i * 128:(i + 1) * 128], bO.b) for i in range(4)]
            bO2h = [T(bO2.ap[:, i * 128:(i + 1) * 128], bO2.b) for i in range(4)]
            ctr = {"ld": 0, "su": 0}

            def loads(spec, s):
                dirn, tb, full = spec
                rows = slice(tb * 128, (tb + 1) * 128)
                L, Hi_ = lf[s], hi[s]
                k.dma(L.ap, (LFF if dirn == 0 else LFB).ap[rows, :], [LFF if dirn == 0 else LFB], [L], L)
                k.dma(Hi_.ap, HI.ap[rows, :], [HI], [Hi_], Hi_)
                if full:
                    k.dma(hq[s].ap, HQ.ap[rows, :], [HQ], [hq[s]], hq[s])
                    if dirn == 0:
                        k.dma(hg[s].ap, HG.ap[rows, :], [HG], [hg[s]], hg[s])
                        k.dma(ob[s].ap, OB.ap[rows, :], [OB], [ob[s]], ob[s])

            def prep(spec, i):
                dirn, tb, full = spec
                s, s2 = i % 3, i % 2
                L = lf[s]
                U = Uf if dirn == 0 else Ub
                JU = JUf if dirn == 0 else JUb
                KHA, KHB, DC = kha[s2], khb[s2], dec[s2]
                qkT, qTa, qTb = qkT2[s2], qTa2[s2], qTb2[s2]
                k.mm(bB.ap, JU, L.ap, True, True, [cst_t, L], [bB])
                for hh in range(4):
                    k.mm(bC.ap[:, hh * 2:hh * 2 + 2], L.ap[:, hh * 128:(hh + 1) * 128], Jsel, True, True, [L, cst_t], [bC], inc=(hh == 3))
                if full:
                    k.mm(bA.ap, U, L.ap, True, True, [cst_t, L], [bA])
                k.act(k_t.ap, L.ap, AF.Exp, [L], [k_t])
                k.ts("dve", k_t.ap, k_t.ap, -1.0, 1.0, ALU.mult, ALU.add, [k_t], [k_t])
                k.act(ed_t.ap, bB.ap, AF.Exp, [bB], [ed_t])
                k.act(DC.ap, bC.ap[:, 0:8], AF.Exp, [bC], [DC])
                k.tt("dve", KHA.ap[0:64, :], k_t.ap[0:64, :], ed_t.ap[0:64, :], ALU.mult, [k_t, ed_t], [KHA])
                k.tt("dve", KHB.ap[64:128, :], k_t.ap[64:128, :], ed_t.ap[64:128, :], ALU.mult, [k_t, ed_t], [KHB])
                if full:
                    k.act(b_t.ap, bA.ap, AF.Exp, [bA], [b_t])
                    k.act(ib_t.ap, bA.ap, AF.Exp, [bA], [ib_t], scale=-1.0)
                    k.tt("dve", qk_.ap[:, 0:512], hq[s].ap, b_t.ap, ALU.mult, [hq[s], b_t], [qk_])
                    k.tt("dve", qk_.ap[:, 512:1024], k_t.ap, ib_t.ap, ALU.mult, [k_t, ib_t, qk_], [qk_])
                    for j in range(8):
                        k.tr(bT.ap[:, j, :], qk_.ap[:, j * 128:(j + 1) * 128], ident.ap, [qk_, ident], [bT], inc=(j == 7))
                    k.copy("act", qkT.ap, bT.ap, [bT], [qkT])
                    k.copy("dve", qTa.ap[:, :, 0:64], qkT.ap[:, 0:4, 0:64], [qkT], [qTa])
                    k.copy("dve", qTb.ap[:, :, 64:128], qkT.ap[:, 0:4, 64:128], [qkT], [qTb])
                    for hh in range(4):
                        k.mm(bSC[hh].ap, qkT.ap[:, 4 + hh, :], qkT.ap[:, hh, :], True, True, [qkT], [bSC[hh]])
                    for hh in range(4):
                        a_ = AT2[s2][hh]
                        k.tt("dve", a_.ap, bSC[hh].ap, U, ALU.mult, [bSC[hh], cst_t], [a_])

            def chain(spec, i, ci):
                dirn, tb, full = spec
                s, s2 = i % 3, i % 2
                Hi_ = hi[s]
                KHA, KHB, DC = kha[s2], khb[s2], dec[s2]
                qTa, qTb = qTa2[s2], qTb2[s2]
                ats = AT2[s2]
                order = (0, 1) if dirn == 0 else (1, 0)
                qfirst, qsecond = (qTa, qTb) if dirn == 0 else (qTb, qTa)
                c = order[ci]
                sus = []
                for hh in range(4):
                    dh = dirn * 4 + hh
                    hc = slice(hh * 128, (hh + 1) * 128)
                    if full:
                        if ci == 0:
                            k.mm(bOh[hh].ap, qfirst.ap[:, hh, :], Sbf[dh].ap, True, False, [qfirst, Sbf[dh]], [bOh[hh]], inc=False)
                            k.mm(bOh[hh].ap, ats[hh].ap, Hi_.ap[:, hc], False, True, [ats[hh], Hi_], [bOh[hh]], inc=True)
                        else:
                            k.mm(bO2h[hh].ap, qsecond.ap[:, hh, :], Sbf[dh].ap, True, True, [qsecond, Sbf[dh]], [bO2h[hh]], inc=True)
                    su = bSU[hh]
                    KHc = KHA if c == 0 else KHB
                    k.mm(su.ap, KHc.ap[:, hc], Hi_.ap[:, hc], True, True, [KHc, Hi_], [su])
                    sus.append(su)
                for hh in range(4):
                    dh = dirn * 4 + hh
                    k.stt(S32[dh].ap, S32[dh].ap, DC.ap[:, hh * 2 + c:hh * 2 + c + 1], sus[hh].ap, ALU.mult, ALU.add,
                          [S32[dh], DC, sus[hh]], [S32[dh]])
                for hh in range(4):
                    dh = dirn * 4 + hh
                    k.copy("act", Sbf[dh].ap, S32[dh].ap, [S32[dh]], [Sbf[dh]])

            def finish(spec, i):
                dirn, tb, full = spec
                s, s2 = i % 3, i % 2
                rows = slice(tb * 128, (tb + 1) * 128)
                if not full:
                    return
                o_ = osb[s2]
                if dirn == 1:
                    k.copy("act", o_.ap, bO.ap, [bO], [o_])
                    k.tt("dve", o_.ap, bO2.ap, o_.ap, ALU.add, [bO2, o_], [o_])
                    k.dma(OB.ap[rows, :], o_.ap, [o_], [OB], o_)
                    return
                k.tt("dve", o_.ap, bO.ap, ob[s].ap, ALU.add, [bO, ob[s]], [o_])
                k.tt("dve", o_.ap, bO2.ap, o_.ap, ALU.add, [bO2, o_], [o_])
                k.act(sq.ap, o_.ap, AF.Square, [o_], [sq])
                k.op("dve", lambda e: e.tensor_reduce(out=ss4.ap, in_=sq.ap.rearrange("p (h c) -> p h c", c=128), axis=AX.X, op=ALU.add),
                     [sq], [ss4])
                k.rstd(ss4.ap, ln4, rs4.ap, 128, [ss4], [rs4])
                for hh in range(4):
                    hc = slice(hh * 128, (hh + 1) * 128)
                    k.stt(on.ap[:, hc], o_.ap[:, hc], rs4.ap[:, hh:hh + 1], gh, ALU.mult, ALU.mult, [o_, rs4, grow_t], [on])
                cc = c2[s2]
                k.tt("dve", cc.ap, on.ap, hg[s].ap, ALU.mult, [on, hg[s]], [cc])
                k.dma(CAT.ap[rows, 512:1024], cc.ap, [cc], [CAT], cc)

            specs = [(1, tb, False) for tb in range(NB - 1, NQ - 1, -1)] + [(1, tb, True) for tb in range(NQ - 1, -1, -1)] \
                + [(0, tb, True) for tb in range(NQ)]
            n = len(specs)
            i_f0 = NB
            loads(specs[0], 0)
            loads(specs[1], 1)
            prep(specs[0], 0)
            for i in range(n):
                chain(specs[i], i, 0)
                early = (i + 1 < n) and (i + 1 != i_f0)
                if early:
                    prep(specs[i + 1], i + 1)
                chain(specs[i], i, 1)
                finish(specs[i], i)
                if i + 1 < n and not early:
                    loads(specs[i + 1], (i + 1) % 3)
                    prep(specs[i + 1], i + 1)
                if i + 2 < n and (i + 2) != i_f0:
                    loads(specs[i + 2], (i + 2) % 3)

        def phase_out():
            wo = k.sb([128, 8, 1024], BF16, "wo")
            k.dma(wo.ap, WOUT.ap.rearrange("(kc p) n -> p kc n", p=128), [WOUT], [wo], wo)
            zt = k.sb([128, 8, 2], BF16, "zt")
            k.memset("pool", zt.ap, 0.0, [zt])
            k.dma(U2T.ap[:, :, 0:1].rearrange("kc p t -> p kc t"), zt.ap[:, :, 0:1], [zt], [U2T], zt, slow=True)
            cat = [k.sb([128, 1024], BF16, "cat%d" % i) for i in range(2)]
            xr = [k.sb([128, 1024], F32, "xr%d" % i) for i in range(2)]
            catT = k.sb([128, 8, 128], BF16, "catT")
            h2 = [k.sb([128, 1024], F32, "h2_%d" % i) for i in range(2)]
            junk = k.sb([128, 1024], BF16, "ojunk")
            ssA = k.sb([128, 2], F32, "ssA")
            ss = k.sb([128, 1], F32, "oss")
            lnv = k.sb([128, 1], F32, "olnv")
            rs = k.sb([128, 1], F32, "ors")
            ss2 = k.sb([128, 1], F32, "oss2")
            rs2 = k.sb([128, 1], F32, "ors2")
            u2 = k.sb([128, 1024], BF16, "u2")
            u2s = [k.sb([128, 8, 128], BF16, "u2s%d" % i) for i in range(2)]
            gpm = grow_t.ap[:, 0:1024]
            bT = T(k.bank_bf(7), bank[7].b)
            bT2 = T(k.bank_bf(6), bank[6].b)

            def load(tb):
                s = tb % 2
                rows = slice(tb * 128, (tb + 1) * 128)
                k.dma(cat[s].ap, CAT.ap[rows, :], [CAT], [cat[s]], cat[s])
                k.dma(xr[s].ap, x[rows, :], [], [xr[s]], xr[s])

            load(0)
            for tb in range(NQ):
                s = tb % 2
                if tb + 1 < NQ:
                    load(tb + 1)
                rows = slice(tb * 128, (tb + 1) * 128)
                m0, m1 = bank[(tb % 2) * 2], bank[(tb % 2) * 2 + 1]
                for kc in range(8):
                    k.tr(bT.ap[:, kc, :], cat[s].ap[:, kc * 128:(kc + 1) * 128], ident.ap, [cat[s], ident], [bT], inc=(kc == 7))
                k.copy("act", catT.ap, bT.ap, [bT], [catT])
                for hf, m in enumerate((m0, m1)):
                    for kc in range(8):
                        k.mm(m.ap, catT.ap[:, kc, :], wo.ap[:, kc, hf * 512:(hf + 1) * 512], kc == 0, kc == 7, [catT, wo], [m], inc=(kc == 7))
                k.act(junk.ap[:, 0:512], m0.ap, AF.Square, [m0], [junk, ssA], accum_out=ssA.ap[:, 0:1])
                k.act(junk.ap[:, 512:1024], m1.ap, AF.Square, [m1, ssA], [junk, ssA], accum_out=ssA.ap[:, 1:2])
                k.tt("dve", ss.ap, ssA.ap[:, 0:1], ssA.ap[:, 1:2], ALU.add, [ssA], [ss])
                k.rstd(ss.ap, lnv, rs.ap, 1024, [ss], [rs])
                hh = h2[s]
                k.stt(hh.ap[:, 0:512], m0.ap, rs.ap, gpm[:, 0:512], ALU.mult, ALU.mult, [m0, rs, grow_t], [hh])
                k.stt(hh.ap[:, 512:1024], m1.ap, rs.ap, gpm[:, 512:1024], ALU.mult, ALU.mult, [m1, rs, grow_t, hh], [hh])
                k.tt("dve", hh.ap, hh.ap, xr[s].ap, ALU.add, [hh, xr[s]], [hh])
                if tb < NO:
                    k.dma(H2.ap[rows, :], hh.ap, [hh], [H2], hh)
                k.act(junk.ap, hh.ap, AF.Square, [hh], [junk, ss2], accum_out=ss2.ap)
                k.rstd(ss2.ap, lnv, rs2.ap, 1024, [ss2], [rs2])
                k.ts("dve", u2.ap, hh.ap, rs2.ap, None, ALU.mult, None, [hh, rs2], [u2])
                for kc in range(8):
                    k.tr(bT2.ap[:, kc, :], u2.ap[:, kc * 128:(kc + 1) * 128], ident.ap, [u2, ident], [bT2], inc=(kc == 7))
                us = u2s[s]
                k.copy("act", us.ap, bT2.ap, [bT2], [us])
                if tb < NO:
                    k.dma(U2T.ap[:, :, 1 + tb * 128:1 + (tb + 1) * 128].rearrange("kc p t -> p kc t"), us.ap, [us], [U2T], us)
                else:
                    k.dma(U2T.ap[:, :, 1 + tb * 128:2 + tb * 128].rearrange("kc p t -> p kc t"), us.ap[:, :, 0:1], [us], [U2T], us, slow=True)

        def phase_ffn():
            wdn = k.sb([128, 32, 1024], BF16, "wdn")
            for i in range(4):
                k.dma(wdn.ap[:, i * 8:(i + 1) * 8, :], WDN.ap[i * 1024:(i + 1) * 1024, :].rearrange("(fc p) n -> p fc n", p=128),
                      [WDN], [wdn], wdn)
            cv = k.sb([128, 64, 4], F32, "convc")
            k.dma(cv.ap, convc.rearrange("p (c t) -> p c t", t=4), [], [cv], cv)
            u2t = [k.sb([128, 8, 258], BF16, "u2t%d" % i) for i in range(2)]
            wup = [k.sb([128, 2, 8, 128], BF16, "wup%d" % i) for i in range(4)]
            hT = [k.sb([128, 32, 256], BF16, "hT%d" % i) for i in range(2)]
            cg = [k.sb([128, 256], F32, "cg%d" % i) for i in range(3)]
            cu = [k.sb([128, 256], F32, "cu%d" % i) for i in range(3)]
            gl = [k.sb([128, 256], F32, "gl%d" % i) for i in range(3)]
            h2r = [k.sb([128, 1024], F32, "h2r%d" % i) for i in range(2)]
            h3 = [k.sb([128, 1024], F32, "h3_%d" % i) for i in range(2)]
            junk = k.sb([128, 1024], BF16, "fjunk")
            ssA = k.sb([128, 2], F32, "fssA")
            ss = k.sb([128, 1], F32, "fss")
            lnv = k.sb([128, 1], F32, "flnv")
            rs = k.sb([128, 1], F32, "frs")
            gpf = grow_t.ap[:, 1024:2048]
            NTL = NO // 2
            ctr = {"w": 0, "e": 0, "y": 0}
            seq = [(j, fc) for j in range(NTL) for fc in range(32)]

            def load_w(idx):
                j, fc = seq[idx]
                w_ = wup[idx % 4]
                k.dma(w_.ap, WUPS.ap[fc], [WUPS], [w_], w_)

            def load_u(j):
                u_ = u2t[j % 2]
                k.dma(u_.ap, U2T.ap[:, :, j * 256:j * 256 + 258].rearrange("kc p t -> p kc t"), [U2T], [u_], u_)

            load_u(0)
            for i in range(3):
                load_w(i)
            for idx, (j, fc) in enumerate(seq):
                if fc == 0 and j + 1 < NTL:
                    load_u(j + 1)
                if idx + 3 < len(seq):
                    load_w(idx + 3)
                u_ = u2t[j % 2]
                w_ = wup[idx % 4]
                hh = hT[j % 2]
                bG, bU = bank[(idx % 2) * 2], bank[(idx % 2) * 2 + 1]
                for g, bb in enumerate((bG, bU)):
                    for kc in range(8):
                        k.mm(bb.ap[:, 0:258], w_.ap[:, g, kc, :], u_.ap[:, kc, :], kc == 0, kc == 7, [w_, u_], [bb], inc=(kc == 7))
                e = ctr["e"] % 3
                ctr["e"] += 1
                for bb, dst, ch in ((bG, cg[e], fc), (bU, cu[e], 32 + fc)):
                    k.act(dst.ap, bb.ap[:, 1:257], AF.Identity, [bb, cv], [dst], scale=cv.ap[:, ch, 1:2], bias=cv.ap[:, ch, 3:4])
                    k.stt(dst.ap, bb.ap[:, 0:256], cv.ap[:, ch, 0:1], dst.ap, ALU.mult, ALU.add, [bb, cv, dst], [dst])
                    k.stt(dst.ap, bb.ap[:, 2:258], cv.ap[:, ch, 2:3], dst.ap, ALU.mult, ALU.add, [bb, cv, dst], [dst])
                k.act(gl[e].ap, cg[e].ap, AF.Gelu_apprx_tanh, [cg[e]], [gl[e]])
                k.tt("dve", hh.ap[:, fc, :], gl[e].ap, cu[e].ap, ALU.mult, [gl[e], cu[e]], [hh])
                if fc == 31:
                    for tbb in range(2):
                        tb = j * 2 + tbb
                        rows = slice(tb * 128, (tb + 1) * 128)
                        ys = ctr["y"] % 2
                        ctr["y"] += 1
                        y0, y1 = bank[4 + ys * 2], bank[5 + ys * 2]
                        k.dma(h2r[ys].ap, H2.ap[rows, :], [H2], [h2r[ys]], h2r[ys])
                        for f2 in range(32):
                            for hf, yb in enumerate((y0, y1)):
                                k.mm(yb.ap, hh.ap[:, f2, tbb * 128:(tbb + 1) * 128], wdn.ap[:, f2, hf * 512:(hf + 1) * 512],
                                     f2 == 0, f2 == 31, [hh, wdn], [yb], inc=(f2 == 31))
                        k.act(junk.ap[:, 0:512], y0.ap, AF.Square, [y0], [junk, ssA], accum_out=ssA.ap[:, 0:1])
                        k.act(junk.ap[:, 512:1024], y1.ap, AF.Square, [y1, ssA], [junk, ssA], accum_out=ssA.ap[:, 1:2])
                        k.tt("dve", ss.ap, ssA.ap[:, 0:1], ssA.ap[:, 1:2], ALU.add, [ssA], [ss])
                        k.rstd(ss.ap, lnv, rs.ap, 1024, [ss], [rs])
                        o_ = h3[ys]
                        k.stt(o_.ap[:, 0:512], y0.ap, rs.ap, gpf[:, 0:512], ALU.mult, ALU.mult, [y0, rs, grow_t], [o_])
                        k.stt(o_.ap[:, 512:1024], y1.ap, rs.ap, gpf[:, 512:1024], ALU.mult, ALU.mult, [y1, rs, grow_t, o_], [o_])
                        k.tt("dve", o_.ap, o_.ap, h2r[ys].ap, ALU.add, [o_, h2r[ys]], [o_])
                        k.dma(H3.ap[rows, :], o_.ap, [o_], [H3], o_)

        def phase_fin():
            wg = k.sb([128, 8, 1024], BF16, "wg")
            k.dma(wg.ap, WG.ap.rearrange("(kc p) n -> p kc n", p=128), [WG], [wg], wg)
            wp = k.sb([128, 2, 1024], BF16, "wp")
            k.dma(wp.ap, WPLE.ap.rearrange("(kc p) n -> p kc n", p=128), [WPLE], [wp], wp)
            h3 = [k.sb([128, 1024], F32, "fh3_%d" % i) for i in range(3)]
            pp = [k.sb([128, 256], F32, "pp%d" % i) for i in range(3)]
            hb = [k.sb([128, 1024], BF16, "hb%d" % i) for i in range(2)]
            pb = [k.sb([128, 256], BF16, "pb%d" % i) for i in range(2)]
            h3T = [k.sb([128, 8, 128], BF16, "h3T%d" % i) for i in range(2)]
            pT = [k.sb([128, 2, 128], BF16, "pT%d" % i) for i in range(2)]
            den = k.sb([128, 1024], F32, "den")
            ev = k.sb([128, 1024], F32, "ev")
            ost = [k.sb([128, 1024], F32, "ost%d" % i) for i in range(2)]
            junk = k.sb([128, 1024], BF16, "njunk")
            ssA = k.sb([128, 2], F32, "nssA")
            ss = k.sb([128, 1], F32, "nss")
            lnv = k.sb([128, 1], F32, "nlnv")
            rs = k.sb([128, 1], F32, "nrs")
            gpl = grow_t.ap[:, 2048:3072]
            bT = T(k.bank_bf(7), bank[7].b)
            bP = T(k.bank_bf(6)[:, 0:2, :], bank[6].b)
            g0, g1, e0, e1 = bank[0], bank[1], bank[2], bank[3]

            def load(tb):
                s = tb % 3
                rows = slice(tb * 128, (tb + 1) * 128)
                k.dma(h3[s].ap, H3.ap[rows, :], [H3], [h3[s]], h3[s])
                k.dma(pp[s].ap, p_in[rows, :], [], [pp[s]], pp[s])

            def stage_a(tb):
                s, d = tb % 3, tb % 2
                k.copy("dve", hb[d].ap, h3[s].ap, [h3[s]], [hb[d]])
                k.copy("dve", pb[d].ap, pp[s].ap, [pp[s]], [pb[d]])
                for kc in range(8):
                    k.tr(bT.ap[:, kc, :], hb[d].ap[:, kc * 128:(kc + 1) * 128], ident.ap, [hb[d], ident], [bT], inc=(kc == 7))
                for kc in range(2):
                    k.tr(bP.ap[:, kc, :], pb[d].ap[:, kc * 128:(kc + 1) * 128], ident.ap, [pb[d], ident], [bP], inc=(kc == 1))
                k.copy("act", h3T[d].ap, bT.ap, [bT], [h3T[d]])
                k.copy("act", pT[d].ap, bP.ap, [bP], [pT[d]])

            def stage_mm(tb):
                d = tb % 2
                for hf, gb in enumerate((g0, g1)):
                    for kc in range(8):
                        k.mm(gb.ap, h3T[d].ap[:, kc, :], wg.ap[:, kc, hf * 512:(hf + 1) * 512], kc == 0, kc == 7, [h3T[d], wg], [gb], inc=(kc == 7))
                for hf, eb in enumerate((e0, e1)):
                    for kc in range(2):
                        k.mm(eb.ap, pT[d].ap[:, kc, :], wp.ap[:, kc, hf * 512:(hf + 1) * 512], kc == 0, kc == 1, [pT[d], wp], [eb], inc=(kc == 1))

            def stage_b(tb):
                s = tb % 3
                rows = slice(tb * 128, (tb + 1) * 128)
                k.act(den.ap[:, 0:512], g0.ap, AF.Exp, [g0], [den], scale=-1.0)
                k.act(den.ap[:, 512:1024], g1.ap, AF.Exp, [g1, den], [den], scale=-1.0)
                k.act(junk.ap[:, 0:512], e0.ap, AF.Square, [e0], [junk, ssA], accum_out=ssA.ap[:, 0:1])
                k.act(junk.ap[:, 512:1024], e1.ap, AF.Square, [e1, ssA], [junk, ssA], accum_out=ssA.ap[:, 1:2])
                k.tt("dve", ss.ap, ssA.ap[:, 0:1], ssA.ap[:, 1:2], ALU.add, [ssA], [ss])
                k.rstd(ss.ap, lnv, rs.ap, 1024, [ss], [rs])
                k.stt(ev.ap[:, 0:512], e0.ap, rs.ap, gpl[:, 0:512], ALU.mult, ALU.mult, [e0, rs, grow_t], [ev])
                k.stt(ev.ap[:, 512:1024], e1.ap, rs.ap, gpl[:, 512:1024], ALU.mult, ALU.mult, [e1, rs, grow_t, ev], [ev])
                k.act(den.ap, den.ap, AF.Ln, [den], [den], bias=1.0)
                k.act(den.ap, den.ap, AF.Exp, [den], [den], scale=-1.0)
                k.tt("dve", ev.ap, ev.ap, den.ap, ALU.mult, [ev, den], [ev])
                o_ = ost[tb % 2]
                k.tt("dve", o_.ap, ev.ap, h3[s].ap, ALU.add, [ev, h3[s]], [o_])
                k.dma(out_d[rows, :], o_.ap, [o_], [], o_)

            load(0)
            if NO > 1:
                load(1)
            stage_a(0)
            stage_mm(0)
            for tb in range(NO):
                if tb + 2 < NO:
                    load(tb + 2)
                if tb + 1 < NO:
                    stage_a(tb + 1)
                stage_b(tb)
                if tb + 1 < NO:
                    stage_mm(tb + 1)

        mj = phase_p0()
        phase_pa(mj)
        phases = [phase_att, phase_hg, phase_out, phase_ffn, phase_fin]
        for i, ph in enumerate(phases[:max(0, NPH - 2)]):
            new_phase()
            ph()
        k.barrier()
        k.emit_all()
    return nc


def _t5_bucket(rel):
    half, max_exact = 16, 8
    n = np.abs(rel)
    scaled = np.log(np.maximum(n, 1).astype(np.float32) / np.float32(max_exact)) / np.float32(math.log(128 / max_exact))
    large = np.minimum(max_exact + (scaled * np.float32(half - max_exact)).astype(np.int32), half - 1)
    return np.where(rel > 0, half, 0) + np.where(n < max_exact, n, large)


def _consts():
    c = np.zeros((128, 904), np.float32)
    s = np.arange(128)[:, None]
    t = np.arange(128)[None, :]
    same = (s // 64) == (t // 64)
    c[:, 0:128] = np.eye(128, dtype=np.float32)
    Uf = (same & (s <= t)).astype(np.float32)
    Ub = (same & (s >= t)).astype(np.float32)
    J = same.astype(np.float32)
    c[:, 128:256] = Uf
    c[:, 256:384] = Ub
    c[:, 384:512] = J - Uf
    c[:, 512:640] = J - Ub
    c[:, 896] = (np.arange(128) < 64)
    c[:, 897] = (np.arange(128) >= 64)
    return c


def make_in_maps(inputs, n_seq, NB, NO):
    f = lambda a: np.ascontiguousarray(a, dtype=np.float32)
    x, p = np.asarray(inputs["x"]), np.asarray(inputs["p"])
    rel_bias = np.asarray(inputs["rel_bias"])
    w_in = np.asarray(inputs["w_in"])[0]
    lb_param = np.asarray(inputs["lb_param"])[:, 0:2, :]
    conv_w = np.asarray(inputs["conv_w"])[0]
    conv_b = np.asarray(inputs["conv_b"])[0]
    bc = lambda v: np.broadcast_to(np.asarray(v, np.float32).reshape(1, -1), (128, np.asarray(v).size))
    grow = np.concatenate([bc(inputs["g_post_mix"][0]), bc(inputs["g_post_ffn"][0]), bc(inputs["g_ple"][0]),
                           bc(inputs["g_diff"][0]), bc(inputs["g_hgrn"][0])], axis=1)
    gcols = np.concatenate([np.asarray(inputs["g_pre_mix"][0]).reshape(8, 128).T,
                            np.asarray(inputs["g_pre_ffn"][0]).reshape(8, 128).T], axis=1)
    lamv = np.concatenate([bc(inputs["lambda_q1"][0]), bc(inputs["lambda_k1"][0]), bc(inputs["lambda_q2"][0]),
                           bc(inputs["lambda_k2"][0])], axis=1)
    cst = _consts()
    kk = np.arange(128)[:, None, None]
    dd = np.arange(9)[None, :, None] - 4
    qq = np.arange(128)[None, None, :]
    rel = dd * 128 + kk - qq
    shared = {
        "w_out": f(inputs["w_out"][0]), "w_up": f(inputs["w_up"][0]), "w_down": f(inputs["w_down"][0]),
        "w_ple": f(inputs["w_ple"][0]), "w_gate": f(inputs["w_ple_gate"][0]),
        "gcols": f(gcols), "grow": f(grow), "lamv": f(lamv), "cst": f(cst),
    }
    maps = []
    for c in range(2 * n_seq):
        b, rev = c // 2, c % 2
        xs, ps = x[b], p[0, b]
        wi, lbq, cw = w_in, lb_param, conv_w
        rl = rel
        if rev:
            xs, ps = xs[::-1], ps[::-1]
            wi = np.concatenate([w_in[:, :2560], w_in[:, 3072:3584], w_in[:, 2560:3072], w_in[:, 3584:]], axis=1)
            lbq = lb_param[::-1]
            cw = conv_w[::-1]
            rl = -rel
        bstrip = rel_bias[_t5_bucket(rl.astype(np.int32))]
        bstrip = np.transpose(bstrip, (3, 0, 1, 2)).reshape(4, 128, 9 * 128)
        convc = np.concatenate([cw.reshape(3, 64, 128).transpose(2, 1, 0), conv_b.reshape(64, 128).T[:, :, None]], axis=2)
        m = dict(shared)
        m.update({
            "x": f(xs[:NB * 128]), "p": f(ps[:NO * 128]), "w_in": f(wi),
            "lbp": f(np.broadcast_to(lbq.reshape(1, -1), (128, 2048))),
            "convc": f(convc.reshape(128, 256)), "bstrip": f(bstrip),
        })
        maps.append(m)
    return maps


_CACHE = {}
_DBG = ()
_NPH = 7
_LAST = {}


def kernel(**inputs):
    inputs = {k_: np.asarray(v) for k_, v in inputs.items()}
    x = inputs["x"]
    n_seq, tlen = x.shape[0], x.shape[1]
    NB = tlen // 128
    NO = NB // 2
    key = (NB, NO)
    if key not in _CACHE:
        _CACHE[key] = build(NB, NO, dbg=_DBG, NPH=_NPH)
    nc = _CACHE[key]
    maps = make_in_maps(inputs, n_seq, NB, NO)
    res = run_bass_kernel_spmd(nc, maps, core_ids=list(range(2 * n_seq)))
    _LAST["res"] = res
    out = np.empty((n_seq, tlen, D), np.float32)
    half = NO * 128
    for c in range(2 * n_seq):
        o = np.asarray(res.results[c]["out"])
        if c % 2 == 0:
            out[c // 2, :half] = o
        else:
            out[c // 2, half:] = o[::-1]
    return out
```

```python
import math
import numpy as np
import concourse.bass as bass
import concourse.mybir as mybir
from concourse.bass_utils import run_bass_kernel_spmd
from contextlib import ExitStack

F32 = mybir.dt.float32
BF16 = mybir.dt.bfloat16
AF = mybir.ActivationFunctionType
ALU = mybir.AluOpType
AX = mybir.AxisListType

D = 1024
EPS = 1e-6
SAME_ENGINE_WAITS = True
_ENVCFG = {}
ARENA_F32 = 49000


class Buf:
    __slots__ = ("name", "w", "r")

    def __init__(self, name):
        self.name = name
        self.w = {}
        self.r = {}


class T:
    __slots__ = ("ap", "b")

    def __init__(self, ap, b):
        self.ap = ap
        self.b = b


def _bufs(lst):
    out = []
    for x in lst:
        if x is None:
            continue
        out.append(x.b if isinstance(x, T) else x)
    return out


class K:
    ENG = ("pe", "act", "dve", "pool", "sp")

    def __init__(self, nc, es):
        self.nc = nc
        self.es = es
        self.sem = {e: es.enter_context(nc.semaphore("cs_" + e)) for e in self.ENG}
        self.semobj = {("c", e): self.sem[e] for e in self.ENG}
        self.cnt = {e: 0 for e in self.ENG}
        self.seen = {e: {} for e in self.ENG}
        self.prog = {e: [] for e in self.ENG}
        self.dcnt = {}
        self.nsem = 0
        self.arena = es.enter_context(nc.sbuf_tensor("arena", [128, ARENA_F32], F32))
        self.ps = es.enter_context(nc.psum_tensor("ps", [128, 8, 512], F32))
        self.off = 0
        self.bank = [T(self.ps[:, i, :], Buf("bank%d" % i)) for i in range(8)]
        self.nb = 0

    def sb(self, shape, dt, name=None):
        n = 1
        for s in shape[1:]:
            n *= s
        words = (n * (2 if dt == BF16 else 4) + 3) // 4
        words = (words + 7) // 8 * 8
        assert self.off + words <= ARENA_F32, ("SBUF arena overflow", self.off, words, name)
        ap = self.arena[0:shape[0], self.off:self.off + words]
        self.off += words
        if dt == BF16:
            ap = ap.bitcast(BF16)[:, 0:n]
        else:
            ap = ap[:, 0:n]
        if len(shape) == 3:
            ap = ap.rearrange("p (a b) -> p a b", b=shape[2])
        elif len(shape) == 4:
            ap = ap.rearrange("p (a b c) -> p a b c", b=shape[2], c=shape[3])
        self.nb += 1
        return T(ap, Buf(name or ("t%d" % self.nb)))

    def bank_bf(self, i, a=8, b=128):
        return self.ps[:, i, :].bitcast(BF16).rearrange("p (a b) -> p a b", b=b)

    def _waits(self, e, deps):
        own = ("c", e)
        for key, val in deps.items():
            if key == own:
                if e == "pe" or e == "sp" or not SAME_ENGINE_WAITS or val > self.cnt[e]:
                    continue
            if self.seen[e].get(key, 0) >= val:
                continue
            self.seen[e][key] = val
            so = self.semobj[key]
            self.prog[e].append(("w", so, val))

    @staticmethod
    def _merge(d, src):
        for k_, v in src.items():
            if d.get(k_, 0) < v:
                d[k_] = v

    def op(self, e, emit, reads=(), writes=(), inc=True):
        R = _bufs(reads)
        W = _bufs(writes)
        deps = {}
        for b in R:
            self._merge(deps, b.w)
        for b in W:
            self._merge(deps, b.w)
            self._merge(deps, b.r)
        self._waits(e, deps)
        val = self.cnt[e] + 1
        if inc:
            self.cnt[e] = val
        self.prog[e].append(("i", emit, self.sem[e] if inc else None, 1))
        key = ("c", e)
        for b in R:
            if b.r.get(key, 0) < val:
                b.r[key] = val
        for b in W:
            b.w = {key: val}
            b.r = {}

    def dma(self, out, in_, reads=(), writes=(), slot=None, q="sp", slow=False):
        R = _bufs(reads)
        W = _bufs(writes)
        sl = slot.b if isinstance(slot, T) else slot
        deps = {}
        for b in R:
            self._merge(deps, b.w)
        for b in W:
            self._merge(deps, b.w)
            self._merge(deps, b.r)
        self._waits(q, deps)
        key = ("d", id(sl))
        if key not in self.semobj:
            self.nsem += 1
            self.semobj[key] = self.es.enter_context(self.nc.semaphore("ds%d" % self.nsem))
            self.dcnt[key] = 0
        self.dcnt[key] += 16
        val = self.dcnt[key]
        so = self.semobj[key]
        if slow:
            self.prog[q].append(("i", lambda eng: eng.dma_start(out=out, in_=in_, allow_slow_non_contiguous=True), so, 16))
        else:
            self.prog[q].append(("i", lambda eng: eng.dma_start(out=out, in_=in_), so, 16))
        for b in R:
            if b.r.get(key, 0) < val:
                b.r[key] = val
        for b in W:
            if b is sl:
                b.w = {key: val}
                b.r = {}
            else:
                b.w[key] = val

    def barrier(self, engines=None):
        allv = {}
        for e in self.ENG:
            if self.cnt[e] > 0:
                allv[("c", e)] = self.cnt[e]
        for key, v in self.dcnt.items():
            if v > 0:
                allv[key] = v
        for e in (engines or self.ENG):
            deps = {k_: v for k_, v in allv.items() if k_ != ("c", e)}
            self._waits(e, deps)

    def emit_all(self):
        nc = self.nc
        block = self.es.enter_context(nc.Block())
        prog = self.prog

        def run(eng, lst):
            for it in lst:
                if it[0] == "w":
                    eng.wait_ge(it[1], it[2])
                else:
                    ins = it[1](eng)
                    if it[2] is not None:
                        ins.then_inc(it[2], it[3])

        @block.sync
        def _(eng):
            run(eng, prog["sp"])

        @block.scalar
        def _(eng):
            run(eng, prog["act"])

        @block.vector
        def _(eng):
            run(eng, prog["dve"])

        @block.gpsimd
        def _(eng):
            run(eng, prog["pool"])

        @block.tensor
        def _(eng):
            run(eng, prog["pe"])

    def act(self, out, in_, func, R, W, **kw):
        self.op("act", lambda e: e.activation(out=out, in_=in_, func=func, **kw), R, W)

    def tt(self, eng, out, in0, in1, op, R, W):
        self.op(eng, lambda e: e.tensor_tensor(out=out, in0=in0, in1=in1, op=op), R, W)

    def ts(self, eng, out, in0, s1, s2, op0, op1, R, W):
        if op1 is None:
            self.op(eng, lambda e: e.tensor_scalar(out=out, in0=in0, scalar1=s1, scalar2=None, op0=op0), R, W)
        else:
            self.op(eng, lambda e: e.tensor_scalar(out=out, in0=in0, scalar1=s1, scalar2=s2, op0=op0, op1=op1), R, W)

    def stt(self, out, in0, scalar, in1, op0, op1, R, W):
        self.op("dve", lambda e: e.scalar_tensor_tensor(out=out, in0=in0, scalar=scalar, in1=in1, op0=op0, op1=op1), R, W)

    def copy(self, eng, out, in_, R, W):
        if eng == "act":
            self.op("act", lambda e: e.activation(out=out, in_=in_, func=AF.Copy), R, W)
        else:
            self.op(eng, lambda e: e.tensor_copy(out=out, in_=in_), R, W)

    def recip(self, out, in_, R, W):
        self.op("dve", lambda e: e.reciprocal(out=out, in_=in_), R, W)

    def memset(self, eng, out, val, W):
        self.op(eng, lambda e: e.memset(out, val), [], W)

    def mm(self, out, lhsT, rhs, start, stop, R, W, inc=True):
        self.op("pe", lambda e: e.matmul(out, lhsT=lhsT, rhs=rhs, start=start, stop=stop), R, W, inc=inc)

    def tr(self, out, in_, ident, R, W, inc=True):
        self.op("pe", lambda e: e.transpose(out=out, in_=in_, identity=ident), R, W, inc=inc)

    def rstd(self, ss, tmp, out, n, R, W):
        self.act(tmp.ap, ss, AF.Ln, list(R) + [self.eps], [tmp], scale=1.0 / n, bias=self.eps.ap)
        self.act(out, tmp.ap, AF.Exp, [tmp], W, scale=-0.5)


def build(NB, NO, dbg=(), NPH=7):
    NQ = NO + 1
    TT, TQ, TO = NB * 128, NQ * 128, NO * 128
    nc = bass.Bass("TRN2", target_bir_lowering=False)

    def din(name, shape, dt=F32):
        return nc.dram_tensor(name, list(shape), dt, kind="ExternalInput").ap()

    def dscr(name, shape, dt):
        kind = "ExternalOutput" if name in dbg else "Internal"
        return T(nc.dram_tensor(name, list(shape), dt, kind=kind).ap(), Buf(name))

    x = din("x", [TT, D])
    p_in = din("p", [TO, 256])
    w_in = din("w_in", [D, 4096])
    w_out = din("w_out", [D, D])
    w_up = din("w_up", [D, 8192])
    w_down = din("w_down", [4096, D])
    w_ple = din("w_ple", [256, D])
    w_gate = din("w_gate", [D, D])
    gcols = din("gcols", [128, 16])
    grow = din("grow", [128, 3328])
    lbp = din("lbp", [128, 2048])
    lamv = din("lamv", [128, 256])
    convc = din("convc", [128, 256])
    bstrip = din("bstrip", [4, 128, 9 * 128])
    cst = din("cst", [128, 904])
    out_d = nc.dram_tensor("out", [TO, D], F32, kind="ExternalOutput").ap()

    WIN = dscr("WIN", [D, 4096], BF16)
    WOUT = dscr("WOUT", [D, D], BF16)
    WUPS = dscr("WUPS", [32, 128, 2, 8, 128], BF16)
    WDN = dscr("WDN", [4096, D], BF16)
    WPLE = dscr("WPLE", [256, D], BF16)
    WG = dscr("WG", [D, D], BF16)
    QT = dscr("QT", [4, 128, TQ], BF16)
    KT = dscr("KT", [4, 128, TT], BF16)
    VA = dscr("VA", [TT, 4, 132], BF16)
    HQ = dscr("HQ", [TQ, 512], BF16)
    HI = dscr("HI", [TT, 512], BF16)
    HG = dscr("HG", [TQ, 512], BF16)
    LFF = dscr("LFF", [TQ, 512], F32)
    LFB = dscr("LFB", [TT, 512], F32)
    OB = dscr("OB", [TQ, 512], F32)
    CAT = dscr("CAT", [TQ, D], BF16)
    H2 = dscr("H2", [TO, D], F32)
    U2T = dscr("U2T", [8, 128, TO + 2], BF16)
    H3 = dscr("H3", [TO, D], F32)

    es = ExitStack()
    with es:
        k = K(nc, es)
        bank = k.bank
        cst_t = k.sb([128, 904], F32, "cst")
        grow_t = k.sb([128, 3328], F32, "grow")
        ident = k.sb([128, 128], BF16, "ident")
        k.eps = k.sb([128, 1], F32, "eps")
        k.dma(cst_t.ap, cst, [], [cst_t], cst_t)
        k.dma(grow_t.ap, grow, [], [grow_t], grow_t)
        k.copy("dve", ident.ap, cst_t.ap[:, 0:128], [cst_t], [ident])
        k.memset("pool", k.eps.ap, EPS, [k.eps])
        Uf = cst_t.ap[:, 128:256]
        Ub = cst_t.ap[:, 256:384]
        JUf = cst_t.ap[:, 384:512]
        JUb = cst_t.ap[:, 512:640]
        Jsel = cst_t.ap[:, 896:898]
        base_off = k.off

        def new_phase():
            k.barrier()
            k.off = base_off

        def phase_p0():
            gc = k.sb([128, 16], F32, "gcols")
            k.dma(gc.ap, gcols, [], [gc], gc)
            stf = [k.sb([128, 2048], F32, "stf%d" % i) for i in range(4)]
            stb = [k.sb([128, 2048], BF16, "stb%d" % i) for i in range(4)]
            jobs = []
            for kc in range(8):
                for cp in range(2):
                    jobs.append((w_in[kc * 128:(kc + 1) * 128, cp * 2048:(cp + 1) * 2048], 2048,
                                 WIN, WIN.ap[kc * 128:(kc + 1) * 128, cp * 2048:(cp + 1) * 2048], None, gc.ap[:, kc:kc + 1]))
            for kc in range(8):
                jobs.append((w_out[kc * 128:(kc + 1) * 128, :], 1024, WOUT, WOUT.ap[kc * 128:(kc + 1) * 128, :], None, None))
            for kc in range(8):
                for cp in range(4):
                    g = cp // 2
                    fc0 = (cp % 2) * 16
                    dst = WUPS.ap[fc0:fc0 + 16, :, g, kc, :].rearrange("fc p n -> p fc n")
                    jobs.append((w_up[kc * 128:(kc + 1) * 128, cp * 2048:(cp + 1) * 2048], 2048, WUPS, dst, 128, gc.ap[:, 8 + kc:9 + kc]))
            for rc in range(32):
                jobs.append((w_down[rc * 128:(rc + 1) * 128, :], 1024, WDN, WDN.ap[rc * 128:(rc + 1) * 128, :], None, None))
            for rc in range(2):
                jobs.append((w_ple[rc * 128:(rc + 1) * 128, :], 1024, WPLE, WPLE.ap[rc * 128:(rc + 1) * 128, :], None, None))
            for kc in range(8):
                jobs.append((w_gate[kc * 128:(kc + 1) * 128, :], 1024, WG, WG.ap[kc * 128:(kc + 1) * 128, :], None, None))
            def run_job(i):
                src, w, dbuf, dst, split, sc = jobs[i]
                sf = stf[i % 4]
                sbt = stb[i % 4]
                k.dma(sf.ap[:, 0:w], src, [], [sf], sf)
                if i % 2 == 0:
                    if sc is not None:
                        k.ts("dve", sbt.ap[:, 0:w], sf.ap[:, 0:w], sc, None, ALU.mult, None, [sf, gc], [sbt])
                    else:
                        k.copy("dve", sbt.ap[:, 0:w], sf.ap[:, 0:w], [sf], [sbt])
                else:
                    if sc is not None:
                        k.act(sbt.ap[:, 0:w], sf.ap[:, 0:w], AF.Copy, [sf, gc], [sbt], scale=sc)
                    else:
                        k.copy("act", sbt.ap[:, 0:w], sf.ap[:, 0:w], [sf], [sbt])
                srcap = sbt.ap[:, 0:w]
                if split:
                    srcap = srcap.rearrange("p (a b) -> p a b", b=split)
                k.dma(dst, srcap, [sbt], [dbuf], sbt)

            for i in range(16):
                run_job(i)
            pending = list(range(16, len(jobs)))

            def more_jobs(n):
                for _ in range(n):
                    if pending:
                        run_job(pending.pop(0))
            return more_jobs

        def phase_pa(more_jobs):
            win = k.sb([128, 8, 4096], BF16, "win")
            for kc in range(8):
                k.dma(win.ap[:, kc, :], WIN.ap[kc * 128:(kc + 1) * 128, :], [WIN], [win], win)
            lbr = k.sb([128, 2, 2, 512], F32, "lbr")
            k.dma(lbr.ap, lbp.rearrange("p (a b c) -> p a b c", a=2, b=2), [], [lbr], lbr)
            lbt = k.sb([128, 2, 512], F32, "lbt")
            k.tt("dve", lbt.ap, lbr.ap[:, :, 1, :], lbr.ap[:, :, 0, :], ALU.subtract, [lbr], [lbt])
            k.act(lbt.ap, lbt.ap, AF.Exp, [lbt], [lbt])
            k.ts("dve", lbt.ap, lbt.ap, 1.0, None, ALU.add, None, [lbt], [lbt])
            k.recip(lbt.ap, lbt.ap, [lbt], [lbt])
            xt = [k.sb([128, 1024], F32, "xt%d" % i) for i in range(2)]
            junk = k.sb([128, 1024], BF16, "junk")
            ss = k.sb([128, 1], F32, "ss")
            lnv = k.sb([128, 1], F32, "lnv")
            rs = k.sb([128, 1], F32, "rs")
            xb = k.sb([128, 1024], BF16, "xb")
            uT = [k.sb([128, 8, 128], BF16, "uT%d" % i) for i in range(2)]
            qd = [k.sb([128, 512], BF16, "qd%d" % i) for i in range(2)]
            tq = [k.sb([128, 4, 128], BF16, "tq%d" % i) for i in range(4)]
            vas = [k.sb([128, 4, 132], BF16, "vas%d" % i) for i in range(2)]
            for v in vas:
                k.memset("pool", v.ap[:, :, 128:132], 1.0, [v])
            hst = [k.sb([128, 512], BF16, "hst%d" % i) for i in range(4)]
            lfs = [k.sb([128, 512], F32, "lfs%d" % i) for i in range(3)]
            tmp = [k.sb([128, 512], F32, "ptmp%d" % i) for i in range(6)]
            cnt = {"q": 0, "tq": 0, "va": 0, "h": 0, "lf": 0, "t": 0, "pj": 0}
            bT = T(k.bank_bf(7), bank[7].b)
            bQ = [T(k.bank_bf(5)[:, 0:4, :], bank[5].b), T(k.bank_bf(6)[:, 0:4, :], bank[6].b)]

            def nxt(name, lst):
                i = cnt[name]
                cnt[name] += 1
                return lst[i % len(lst)]

            def load_x(tb):
                k.dma(xt[tb % 2].ap, x[tb * 128:(tb + 1) * 128, :], [], [xt[tb % 2]], xt[tb % 2])

            def norm_t(tb):
                xx = xt[tb % 2]
                u = uT[tb % 2]
                k.act(junk.ap, xx.ap, AF.Square, [xx], [junk, ss], accum_out=ss.ap)
                k.rstd(ss.ap, lnv, rs.ap, 1024, [ss], [rs])
                k.ts("dve", xb.ap, xx.ap, rs.ap, None, ALU.mult, None, [xx, rs], [xb])
                for kc in range(8):
                    k.tr(bT.ap[:, kc, :], xb.ap[:, kc * 128:(kc + 1) * 128], ident.ap, [xb, ident], [bT], inc=(kc == 7))
                k.copy("act", u.ap, bT.ap, [bT], [u])

            load_x(0)
            if NB > 1:
                load_x(1)
            norm_t(0)
            for tb in range(NB):
                own = tb < NQ
                if tb + 2 < NB:
                    load_x(tb + 2)
                if tb + 1 < NB:
                    norm_t(tb + 1)
                more_jobs(2)
                u = uT[tb % 2]
                rows = slice(tb * 128, (tb + 1) * 128)
                pairs = [(3, 7), (5, 6), (0, 1), (2, 4)] if own else [(6, 1), (2, 4)]

                def stages_for(g, pj):
                    st = []
                    if g in (0, 1):
                        q_ = nxt("q", qd)
                        bq = bQ[g]
                        t_ = nxt("tq", tq)
                        dst = QT if g == 0 else KT
                        st.append(lambda: k.copy("dve", q_.ap, pj.ap, [pj], [q_]))

                        def trs():
                            for h in range(4):
                                k.tr(bq.ap[:, h, :], q_.ap[:, h * 128:(h + 1) * 128], ident.ap, [q_, ident], [bq], inc=(h == 3))
                        st.append(trs)
                        st.append(lambda: k.copy("act", t_.ap, bq.ap, [bq], [t_]))
                        st.append(lambda: k.dma(dst.ap[:, :, tb * 128:(tb + 1) * 128].rearrange("h p t -> p h t"), t_.ap, [t_], [dst], t_))
                    elif g == 2:
                        v_ = nxt("va", vas)
                        st.append(lambda: k.copy("act", v_.ap[:, :, 0:128], pj.ap.rearrange("p (h c) -> p h c", c=128), [pj], [v_]))
                        st.append(lambda: k.dma(VA.ap[rows, :, :], v_.ap, [v_], [VA], v_))
                    elif g in (3, 7):
                        e_ = nxt("t", tmp)
                        h_ = nxt("h", hst)
                        dst = HQ if g == 3 else HG
                        st.append(lambda: k.act(e_.ap, pj.ap, AF.Exp, [pj], [e_], scale=-1.0))
                        st.append(lambda: k.ts("dve", e_.ap, e_.ap, 1.0, None, ALU.add, None, [e_], [e_]))
                        st.append(lambda: k.recip(e_.ap, e_.ap, [e_], [e_]))
                        st.append(lambda: k.tt("dve", h_.ap, pj.ap, e_.ap, ALU.mult, [pj, e_], [h_]))
                        st.append(lambda: k.dma(dst.ap[rows, :], h_.ap, [h_], [dst], h_))
                    elif g == 4:
                        h_ = nxt("h", hst)
                        st.append(lambda: k.copy("dve", h_.ap, pj.ap, [pj], [h_]))
                        st.append(lambda: k.dma(HI.ap[rows, :], h_.ap, [h_], [HI], h_))
                    else:
                        d_ = g - 5
                        e_ = nxt("t", tmp)
                        a_ = nxt("t", tmp)
                        l_ = nxt("lf", lfs)
                        dst = LFF if g == 5 else LFB
                        st.append(lambda: k.act(e_.ap, pj.ap, AF.Exp, [pj], [e_], scale=-1.0))
                        st.append(lambda: k.tt("dve", a_.ap, e_.ap, lbt.ap[:, d_, :], ALU.mult, [e_, lbt], [a_]))
                        st.append(lambda: k.act(a_.ap, a_.ap, AF.Ln, [a_], [a_], bias=1.0))
                        st.append(lambda: k.act(e_.ap, e_.ap, AF.Ln, [e_], [e_], bias=1.0))
                        st.append(lambda: k.tt("dve", l_.ap, a_.ap, e_.ap, ALU.subtract, [a_, e_], [l_]))
                        st.append(lambda: k.dma(dst.ap[rows, :], l_.ap, [l_], [dst], l_))
                    return st

                for pair in pairs:
                    sts = []
                    for g in pair:
                        pj = bank[cnt["pj"] % 5]
                        cnt["pj"] += 1
                        for kc in range(8):
                            k.mm(pj.ap, u.ap[:, kc, :], win.ap[:, kc, g * 512:(g + 1) * 512], kc == 0, kc == 7,
                                 [u, win], [pj], inc=(kc == 7))
                        sts.append(stages_for(g, pj))
                    for i in range(max(len(x_) for x_ in sts)):
                        for st in sts:
                            if i < len(st):
                                st[i]()
            more_jobs(1000)

        def phase_att():
            bs = k.sb([128, 4, 9, 128], F32, "bs")
            k.dma(bs.ap, bstrip.rearrange("h p (d q) -> p h d q", q=128), [], [bs], bs)
            lv = k.sb([128, 256], F32, "lamv")
            k.dma(lv.ap, lamv, [], [lv], lv)
            pr = k.sb([128, 2, 64], F32, "lprod")
            k.tt("dve", pr.ap[:, 0, :], lv.ap[:, 0:64], lv.ap[:, 64:128], ALU.mult, [lv], [pr])
            k.tt("dve", pr.ap[:, 1, :], lv.ap[:, 128:192], lv.ap[:, 192:256], ALU.mult, [lv, pr], [pr])
            s12 = k.sb([128, 2], F32, "s12")
            k.op("dve", lambda e: e.tensor_reduce(out=s12.ap, in_=pr.ap, axis=AX.X, op=ALU.add), [pr], [s12])
            k.act(s12.ap, s12.ap, AF.Exp, [s12], [s12])
            nlam = k.sb([128, 1], F32, "nlam")
            k.tt("dve", nlam.ap, s12.ap[:, 0:1], s12.ap[:, 1:2], ALU.subtract, [s12], [nlam])
            k.ts("dve", nlam.ap, nlam.ap, -1.0, -0.2, ALU.mult, ALU.add, [nlam], [nlam])
            gd8 = k.sb([128, 128], F32, "gd8")
            k.ts("dve", gd8.ap, grow_t.ap[:, 3072:3200], 0.8, None, ALU.mult, None, [grow_t], [gd8])
            kt = [k.sb([128, TT], BF16, "kt%d" % i) for i in range(2)]
            va = [k.sb([128, NB, 132], BF16, "va%d" % i) for i in range(2)]
            qt = [[k.sb([128, TQ], BF16, "qt%d_%d" % (c, i)) for i in range(2)] for c in range(2)]
            for c in range(2):
                for i in range(2):
                    k.memset("dve", qt[c][i].ap, 0.0, [qt[c][i]])
            PT = [k.sb([128, 2, 512], BF16, "PT%d" % i) for i in range(3)]
            bs8 = k.sb([128, 4, 9, 128], BF16, "bs8")
            k.ts("dve", bs8.ap, bs.ap, 8.0, None, ALU.mult, None, [bs], [bs8])
            cst_ = [k.sb([128, 128], BF16, "cst%d" % i) for i in range(4)]
            a0 = [k.sb([128, 128], F32, "a0%d" % i) for i in range(2)]
            aj = k.sb([128, 128], BF16, "ajunk")
            rc = [k.sb([128, 2], F32, "rc%d" % i) for i in range(2)]
            ssq = [k.sb([128, 1], F32, "ssq%d" % i) for i in range(2)]
            lnq = [k.sb([128, 1], F32, "lnq%d" % i) for i in range(2)]
            rsq = [k.sb([128, 1], F32, "rsq%d" % i) for i in range(2)]

            def load_head(h):
                s = h % 2
                npc = 4 if TT >= 2048 else 1
                w = TT // npc
                for i in range(npc):
                    k.dma(kt[s].ap[:, i * w:(i + 1) * w], KT.ap[h, :, i * w:(i + 1) * w], [KT], [kt[s]], kt[s])
                k.dma(va[s].ap, VA.ap[:, h, :].rearrange("(j p) c -> p j c", p=128), [VA], [va[s]], va[s])
                k.dma(qt[0][s].ap[0:64, :], QT.ap[h, 0:64, :], [QT], [qt[0][s]], qt[0][s])
                k.dma(qt[1][s].ap[64:128, :], QT.ap[h, 64:128, :], [QT], [qt[1][s]], qt[1][s])

            groups = []
            i = 0
            while i < NQ:
                nq = min(2, NQ - i)
                groups.append((i, nq))
                i += nq
            NKP = NB // 2
            steps = [(h, gi, kp) for h in range(4) for gi in range(len(groups)) for kp in range(NKP)]
            ctr = {"pt": 0, "nb": 0, "cs": 0, "ep": 0}

            def S_bank(si, c):
                return bank[(si % 2) * 2 + c]

            def is_near(si):
                h, gi, kp = steps[si]
                i0, nq = groups[gi]
                return not (2 * kp + 1 <= i0 - 2 or 2 * kp >= i0 + nq - 1 + 2)

            def qk(si):
                h, gi, kp = steps[si]
                i0, nq = groups[gi]
                N = nq * 128
                s = h % 2
                near = is_near(si)
                for c in range(2):
                    S = S_bank(si, c)
                    for jj in range(2):
                        j = 2 * kp + jj
                        k.mm(S.ap[:, jj * N:(jj + 1) * N], kt[s].ap[:, j * 128:(j + 1) * 128],
                             qt[c][s].ap[:, i0 * 128:i0 * 128 + N], True, not near, [kt[s], qt[c][s]], [S], inc=(jj == 1 and not near))
                        if near:
                            for ii in range(nq):
                                dl = max(-4, min(4, j - (i0 + ii)))
                                sl = slice(jj * N + ii * 128, jj * N + (ii + 1) * 128)
                                k.mm(S.ap[:, sl], ident.ap, bs8.ap[:, h, dl + 4, :], False, ii == nq - 1, [ident, bs8], [S],
                                     inc=(jj == 1 and ii == nq - 1))

            def expv(si):
                h, gi, kp = steps[si]
                i0, nq = groups[gi]
                N = nq * 128
                j0, j1 = 2 * kp, 2 * kp + 1
                imin, imax = i0, i0 + nq - 1
                m = si % 2
                S0, S1 = bank[2 * m], bank[2 * m + 1]
                Sin = k.ps[:, 2 * m:2 * m + 2, 0:2 * N]
                P = PT[ctr["pt"] % len(PT)]
                ctr["pt"] += 1
                if is_near(si):
                    k.act(P.ap[:, :, 0:2 * N], Sin, AF.Exp, [S0, S1], [P], scale=0.125)
                elif j1 <= imin - 2:
                    k.act(P.ap[:, :, 0:2 * N], Sin, AF.Exp, [S0, S1, bs], [P], scale=0.125, bias=bs.ap[:, h, 0, 0:1])
                else:
                    k.act(P.ap[:, :, 0:2 * N], Sin, AF.Exp, [S0, S1, bs], [P], scale=0.125, bias=bs.ap[:, h, 8, 0:1])
                return P

            def pv(si, pts):
                h, gi, kp = steps[si]
                i0, nq = groups[gi]
                N = nq * 128
                s = h % 2
                P = pts
                for c in range(2):
                    for jj in range(2):
                        j = 2 * kp + jj
                        for ii in range(nq):
                            O = bank[4 + ii * 2 + c]
                            k.mm(O.ap[:, 0:129], P.ap[:, c, jj * N + ii * 128:jj * N + (ii + 1) * 128], va[s].ap[:, j, 0:129],
                                 j == 0, j == NB - 1, [P, va[s]], [O], inc=(jj == 1 and ii == nq - 1))

            def epilogue(si):
                h, gi, kp = steps[si]
                i0, nq = groups[gi]
                for ii in range(nq):
                    e = ctr["ep"] % 2
                    ctr["ep"] += 1
                    O0, O1 = bank[4 + ii * 2], bank[4 + ii * 2 + 1]
                    k.recip(rc[e].ap[:, 0:1], O0.ap[:, 128:129], [O0], [rc[e]])
                    k.recip(rc[e].ap[:, 1:2], O1.ap[:, 128:129], [O1, rc[e]], [rc[e]])
                    k.tt("dve", rc[e].ap[:, 1:2], rc[e].ap[:, 1:2], nlam.ap, ALU.mult, [rc[e], nlam], [rc[e]])
                    k.ts("dve", a0[e].ap, O0.ap[:, 0:128], rc[e].ap[:, 0:1], None, ALU.mult, None, [O0, rc[e]], [a0[e]])
                    k.stt(a0[e].ap, O1.ap[:, 0:128], rc[e].ap[:, 1:2], a0[e].ap, ALU.mult, ALU.add, [O1, rc[e], a0[e]], [a0[e]])
                    k.act(aj.ap, a0[e].ap, AF.Square, [a0[e]], [aj, ssq[e]], accum_out=ssq[e].ap)
                    k.rstd(ssq[e].ap, lnq[e], rsq[e].ap, 128, [ssq[e]], [rsq[e]])
                    cs = cst_[ctr["cs"] % 4]
                    ctr["cs"] += 1
                    k.stt(cs.ap, a0[e].ap, rsq[e].ap, gd8.ap, ALU.mult, ALU.mult, [a0[e], rsq[e], gd8], [cs])
                    r0 = (i0 + ii) * 128
                    k.dma(CAT.ap[r0:r0 + 128, h * 128:(h + 1) * 128], cs.ap, [cs], [CAT], cs)

            load_head(0)
            qk(0)
            for si in range(len(steps)):
                h, gi, kp = steps[si]
                if gi == 0 and kp == 0 and h + 1 < 4:
                    load_head(h + 1)
                if si + 1 < len(steps):
                    qk(si + 1)
                pts = expv(si)
                pv(si, pts)
                if kp == NKP - 1:
                    epilogue(si)

        def phase_hg():
            S32 = [k.sb([128, 128], F32, "S32_%d" % i) for i in range(8)]
            Sbf = [k.sb([128, 128], BF16, "Sbf_%d" % i) for i in range(8)]
            for i in range(8):
                k.memset("pool", S32[i].ap, 0.0, [S32[i]])
                k.memset("pool", Sbf[i].ap, 0.0, [Sbf[i]])
            lf = [k.sb([128, 512], F32, "lf%d" % i) for i in range(2)]
            hq = [k.sb([128, 512], BF16, "hq%d" % i) for i in range(2)]
            hi = [k.sb([128, 512], BF16, "hi%d" % i) for i in range(2)]
            hg = [k.sb([128, 512], BF16, "hg%d" % i) for i in range(2)]
            ob = [k.sb([128, 512], F32, "ob%d" % i) for i in range(2)]
            b_t = k.sb([128, 512], F32, "b_t")
            ib_t = k.sb([128, 512], F32, "ib_t")
            ed_t = k.sb([128, 512], F32, "ed_t")
            k_t = k.sb([128, 512], F32, "k_t")
            qk_ = k.sb([128, 1024], BF16, "qk")
            kha = [k.sb([128, 512], BF16, "kha%d" % i) for i in range(2)]
            khb = [k.sb([128, 512], BF16, "khb%d" % i) for i in range(2)]
            for i in range(2):
                k.memset("dve", kha[i].ap, 0.0, [kha[i]])
                k.memset("dve", khb[i].ap, 0.0, [khb[i]])
            dec = [k.sb([128, 8], F32, "dec%d" % i) for i in range(2)]
            qkT = k.sb([128, 8, 128], BF16, "qkT")
            qTa = k.sb([128, 4, 128], BF16, "qTa")
            qTb = k.sb([128, 4, 128], BF16, "qTb")
            k.memset("pool", qTa.ap, 0.0, [qTa])
            k.memset("pool", qTb.ap, 0.0, [qTb])
            AT = [k.sb([128, 128], BF16, "AT%d" % i) for i in range(4)]
            osb = [k.sb([128, 512], F32, "osb%d" % i) for i in range(2)]
            sq = k.sb([128, 512], F32, "sq")
            ss4 = k.sb([128, 4], F32, "ss4")
            ln4 = k.sb([128, 4], F32, "ln4")
            rs4 = k.sb([128, 4], F32, "rs4")
            on = k.sb([128, 512], F32, "on")
            c2 = [k.sb([128, 512], BF16, "c2_%d" % i) for i in range(2)]
            gh = grow_t.ap[:, 3200:3328]
            bA, bB, bC, bO, bO2 = bank[0], bank[1], bank[2], bank[5], bank[7]
            bT = T(k.bank_bf(3), bank[3].b)
            bSC = [T(bank[4].ap[:, i * 128:(i + 1) * 128], bank[4].b) for i in range(4)]
            bSU = [T(bank[6].ap[:, i * 128:(i + 1) * 128], bank[6].b) for i in range(4)]
            bOh = [T(bO.ap[:, i * 128:(i + 1) * 128], bO.b) for i in range(4)]
            bO2h = [T(bO2.ap[:, i * 128:(i + 1) * 128], bO2.b) for i in range(4)]
            ctr = {"ld": 0, "su": 0}

            def loads(spec, s):
                dirn, tb, full = spec
                rows = slice(tb * 128, (tb + 1) * 128)
                L, Hi_ = lf[s], hi[s]
                k.dma(L.ap, (LFF if dirn == 0 else LFB).ap[rows, :], [LFF if dirn == 0 else LFB], [L], L)
                k.dma(Hi_.ap, HI.ap[rows, :], [HI], [Hi_], Hi_)
                if full:
                    k.dma(hq[s].ap, HQ.ap[rows, :], [HQ], [hq[s]], hq[s])
                    if dirn == 0:
                        k.dma(hg[s].ap, HG.ap[rows, :], [HG], [hg[s]], hg[s])
                        k.dma(ob[s].ap, OB.ap[rows, :], [OB], [ob[s]], ob[s])

            def block(spec, s):
                dirn, tb, full = spec
                rows = slice(tb * 128, (tb + 1) * 128)
                L, Hi_ = lf[s], hi[s]
                U = Uf if dirn == 0 else Ub
                JU = JUf if dirn == 0 else JUb
                KHA, KHB, DC = kha[s], khb[s], dec[s]
                k.mm(bB.ap, JU, L.ap, True, True, [cst_t, L], [bB])
                for hh in range(4):
                    k.mm(bC.ap[:, hh * 2:hh * 2 + 2], L.ap[:, hh * 128:(hh + 1) * 128], Jsel, True, True, [L, cst_t], [bC], inc=(hh == 3))
                if full:
                    k.mm(bA.ap, U, L.ap, True, True, [cst_t, L], [bA])
                k.act(k_t.ap, L.ap, AF.Exp, [L], [k_t])
                k.ts("dve", k_t.ap, k_t.ap, -1.0, 1.0, ALU.mult, ALU.add, [k_t], [k_t])
                k.act(ed_t.ap, bB.ap, AF.Exp, [bB], [ed_t])
                k.act(DC.ap, bC.ap[:, 0:8], AF.Exp, [bC], [DC])
                k.tt("dve", KHA.ap[0:64, :], k_t.ap[0:64, :], ed_t.ap[0:64, :], ALU.mult, [k_t, ed_t], [KHA])
                k.tt("dve", KHB.ap[64:128, :], k_t.ap[64:128, :], ed_t.ap[64:128, :], ALU.mult, [k_t, ed_t], [KHB])
                if full:
                    k.act(b_t.ap, bA.ap, AF.Exp, [bA], [b_t])
                    k.act(ib_t.ap, bA.ap, AF.Exp, [bA], [ib_t], scale=-1.0)
                    k.tt("dve", qk_.ap[:, 0:512], hq[s].ap, b_t.ap, ALU.mult, [hq[s], b_t], [qk_])
                    k.tt("dve", qk_.ap[:, 512:1024], k_t.ap, ib_t.ap, ALU.mult, [k_t, ib_t, qk_], [qk_])
                    for i in range(8):
                        k.tr(bT.ap[:, i, :], qk_.ap[:, i * 128:(i + 1) * 128], ident.ap, [qk_, ident], [bT], inc=(i == 7))
                    k.copy("act", qkT.ap, bT.ap, [bT], [qkT])
                    k.copy("dve", qTa.ap[:, :, 0:64], qkT.ap[:, 0:4, 0:64], [qkT], [qTa])
                    k.copy("dve", qTb.ap[:, :, 64:128], qkT.ap[:, 0:4, 64:128], [qkT], [qTb])
                order = (0, 1) if dirn == 0 else (1, 0)
                qfirst, qsecond = (qTa, qTb) if dirn == 0 else (qTb, qTa)
                ats = []
                if full:
                    for hh in range(4):
                        k.mm(bSC[hh].ap, qkT.ap[:, 4 + hh, :], qkT.ap[:, hh, :], True, True, [qkT], [bSC[hh]])
                    for hh in range(4):
                        a_ = AT[hh]
                        k.tt("dve", a_.ap, bSC[hh].ap, U, ALU.mult, [bSC[hh], cst_t], [a_])
                        ats.append(a_)
                for ci, c in enumerate(order):
                    cr = slice(c * 64, (c + 1) * 64)
                    sus = []
                    for hh in range(4):
                        dh = dirn * 4 + hh
                        hc = slice(hh * 128, (hh + 1) * 128)
                        if full:
                            if ci == 0:
                                k.mm(bOh[hh].ap, qfirst.ap[:, hh, :], Sbf[dh].ap, True, False, [qfirst, Sbf[dh]], [bOh[hh]], inc=False)
                                k.mm(bOh[hh].ap, ats[hh].ap, Hi_.ap[:, hc], False, True, [ats[hh], Hi_], [bOh[hh]], inc=True)
                            else:
                                k.mm(bO2h[hh].ap, qsecond.ap[:, hh, :], Sbf[dh].ap, True, True, [qsecond, Sbf[dh]], [bO2h[hh]], inc=True)
                        su = bSU[hh]
                        KHc = KHA if c == 0 else KHB
                        k.mm(su.ap, KHc.ap[:, hc], Hi_.ap[:, hc], True, True, [KHc, Hi_], [su])
                        sus.append(su)
                    for hh in range(4):
                        dh = dirn * 4 + hh
                        k.stt(S32[dh].ap, S32[dh].ap, DC.ap[:, hh * 2 + c:hh * 2 + c + 1], sus[hh].ap, ALU.mult, ALU.add,
                              [S32[dh], DC, sus[hh]], [S32[dh]])
                    for hh in range(4):
                        dh = dirn * 4 + hh
                        k.copy("act", Sbf[dh].ap, S32[dh].ap, [S32[dh]], [Sbf[dh]])
                if not full:
                    return
                o_ = osb[s]
                if dirn == 1:
                    k.copy("act", o_.ap, bO.ap, [bO], [o_])
                    k.tt("dve", o_.ap, bO2.ap, o_.ap, ALU.add, [bO2, o_], [o_])
                    k.dma(OB.ap[rows, :], o_.ap, [o_], [OB], o_)
                    return
                k.tt("dve", o_.ap, bO.ap, ob[s].ap, ALU.add, [bO, ob[s]], [o_])
                k.tt("dve", o_.ap, bO2.ap, o_.ap, ALU.add, [bO2, o_], [o_])
                k.act(sq.ap, o_.ap, AF.Square, [o_], [sq])
                k.op("dve", lambda e: e.tensor_reduce(out=ss4.ap, in_=sq.ap.rearrange("p (h c) -> p h c", c=128), axis=AX.X, op=ALU.add),
                     [sq], [ss4])
                k.rstd(ss4.ap, ln4, rs4.ap, 128, [ss4], [rs4])
                for hh in range(4):
                    hc = slice(hh * 128, (hh + 1) * 128)
                    k.stt(on.ap[:, hc], o_.ap[:, hc], rs4.ap[:, hh:hh + 1], gh, ALU.mult, ALU.mult, [o_, rs4, grow_t], [on])
                cc = c2[s]
                k.tt("dve", cc.ap, on.ap, hg[s].ap, ALU.mult, [on, hg[s]], [cc])
                k.dma(CAT.ap[rows, 512:1024], cc.ap, [cc], [CAT], cc)

            specs = [(1, tb, False) for tb in range(NB - 1, NQ - 1, -1)] + [(1, tb, True) for tb in range(NQ - 1, -1, -1)] \
                + [(0, tb, True) for tb in range(NQ)]
            n_b1 = NB
            loads(specs[0], 0)
            for i, sp_ in enumerate(specs):
                nx = specs[i + 1] if i + 1 < len(specs) else None
                early = nx is not None and not (nx[0] == 0 and nx[1] == 0)
                if nx is not None and early:
                    loads(nx, (i + 1) % 2)
                block(sp_, i % 2)
                if nx is not None and not early:
                    loads(nx, (i + 1) % 2)

        def phase_out():
            wo = k.sb([128, 8, 1024], BF16, "wo")
            k.dma(wo.ap, WOUT.ap.rearrange("(kc p) n -> p kc n", p=128), [WOUT], [wo], wo)
            zt = k.sb([128, 8, 2], BF16, "zt")
            k.memset("pool", zt.ap, 0.0, [zt])
            k.dma(U2T.ap[:, :, 0:1].rearrange("kc p t -> p kc t"), zt.ap[:, :, 0:1], [zt], [U2T], zt, slow=True)
            cat = [k.sb([128, 1024], BF16, "cat%d" % i) for i in range(2)]
            xr = [k.sb([128, 1024], F32, "xr%d" % i) for i in range(2)]
            catT = k.sb([128, 8, 128], BF16, "catT")
            h2 = [k.sb([128, 1024], F32, "h2_%d" % i) for i in range(2)]
            junk = k.sb([128, 1024], BF16, "ojunk")
            ssA = k.sb([128, 2], F32, "ssA")
            ss = k.sb([128, 1], F32, "oss")
            lnv = k.sb([128, 1], F32, "olnv")
            rs = k.sb([128, 1], F32, "ors")
            ss2 = k.sb([128, 1], F32, "oss2")
            rs2 = k.sb([128, 1], F32, "ors2")
            u2 = k.sb([128, 1024], BF16, "u2")
            u2s = [k.sb([128, 8, 128], BF16, "u2s%d" % i) for i in range(2)]
            gpm = grow_t.ap[:, 0:1024]
            bT = T(k.bank_bf(7), bank[7].b)
            bT2 = T(k.bank_bf(6), bank[6].b)

            def load(tb):
                s = tb % 2
                rows = slice(tb * 128, (tb + 1) * 128)
                k.dma(cat[s].ap, CAT.ap[rows, :], [CAT], [cat[s]], cat[s])
                k.dma(xr[s].ap, x[rows, :], [], [xr[s]], xr[s])

            load(0)
            for tb in range(NQ):
                s = tb % 2
                if tb + 1 < NQ:
                    load(tb + 1)
                rows = slice(tb * 128, (tb + 1) * 128)
                m0, m1 = bank[(tb % 2) * 2], bank[(tb % 2) * 2 + 1]
                for kc in range(8):
                    k.tr(bT.ap[:, kc, :], cat[s].ap[:, kc * 128:(kc + 1) * 128], ident.ap, [cat[s], ident], [bT], inc=(kc == 7))
                k.copy("act", catT.ap, bT.ap, [bT], [catT])
                for hf, m in enumerate((m0, m1)):
                    for kc in range(8):
                        k.mm(m.ap, catT.ap[:, kc, :], wo.ap[:, kc, hf * 512:(hf + 1) * 512], kc == 0, kc == 7, [catT, wo], [m], inc=(kc == 7))
                k.act(junk.ap[:, 0:512], m0.ap, AF.Square, [m0], [junk, ssA], accum_out=ssA.ap[:, 0:1])
                k.act(junk.ap[:, 512:1024], m1.ap, AF.Square, [m1, ssA], [junk, ssA], accum_out=ssA.ap[:, 1:2])
                k.tt("dve", ss.ap, ssA.ap[:, 0:1], ssA.ap[:, 1:2], ALU.add, [ssA], [ss])
                k.rstd(ss.ap, lnv, rs.ap, 1024, [ss], [rs])
                hh = h2[s]
                k.stt(hh.ap[:, 0:512], m0.ap, rs.ap, gpm[:, 0:512], ALU.mult, ALU.mult, [m0, rs, grow_t], [hh])
                k.stt(hh.ap[:, 512:1024], m1.ap, rs.ap, gpm[:, 512:1024], ALU.mult, ALU.mult, [m1, rs, grow_t, hh], [hh])
                k.tt("dve", hh.ap, hh.ap, xr[s].ap, ALU.add, [hh, xr[s]], [hh])
                if tb < NO:
                    k.dma(H2.ap[rows, :], hh.ap, [hh], [H2], hh)
                k.act(junk.ap, hh.ap, AF.Square, [hh], [junk, ss2], accum_out=ss2.ap)
                k.rstd(ss2.ap, lnv, rs2.ap, 1024, [ss2], [rs2])
                k.ts("dve", u2.ap, hh.ap, rs2.ap, None, ALU.mult, None, [hh, rs2], [u2])
                for kc in range(8):
                    k.tr(bT2.ap[:, kc, :], u2.ap[:, kc * 128:(kc + 1) * 128], ident.ap, [u2, ident], [bT2], inc=(kc == 7))
                us = u2s[s]
                k.copy("act", us.ap, bT2.ap, [bT2], [us])
                if tb < NO:
                    k.dma(U2T.ap[:, :, 1 + tb * 128:1 + (tb + 1) * 128].rearrange("kc p t -> p kc t"), us.ap, [us], [U2T], us)
                else:
                    k.dma(U2T.ap[:, :, 1 + tb * 128:2 + tb * 128].rearrange("kc p t -> p kc t"), us.ap[:, :, 0:1], [us], [U2T], us, slow=True)

        def phase_ffn():
            wdn = k.sb([128, 32, 1024], BF16, "wdn")
            for i in range(4):
                k.dma(wdn.ap[:, i * 8:(i + 1) * 8, :], WDN.ap[i * 1024:(i + 1) * 1024, :].rearrange("(fc p) n -> p fc n", p=128),
                      [WDN], [wdn], wdn)
            cv = k.sb([128, 64, 4], F32, "convc")
            k.dma(cv.ap, convc.rearrange("p (c t) -> p c t", t=4), [], [cv], cv)
            u2t = [k.sb([128, 8, 258], BF16, "u2t%d" % i) for i in range(2)]
            wup = [k.sb([128, 2, 8, 128], BF16, "wup%d" % i) for i in range(4)]
            hT = [k.sb([128, 32, 256], BF16, "hT%d" % i) for i in range(2)]
            cg = [k.sb([128, 256], F32, "cg%d" % i) for i in range(3)]
            cu = [k.sb([128, 256], F32, "cu%d" % i) for i in range(3)]
            gl = [k.sb([128, 256], F32, "gl%d" % i) for i in range(3)]
            h2r = [k.sb([128, 1024], F32, "h2r%d" % i) for i in range(2)]
            h3 = [k.sb([128, 1024], F32, "h3_%d" % i) for i in range(2)]
            junk = k.sb([128, 1024], BF16, "fjunk")
            ssA = k.sb([128, 2], F32, "fssA")
            ss = k.sb([128, 1], F32, "fss")
            lnv = k.sb([128, 1], F32, "flnv")
            rs = k.sb([128, 1], F32, "frs")
            gpf = grow_t.ap[:, 1024:2048]
            NTL = NO // 2
            ctr = {"w": 0, "e": 0, "y": 0}
            seq = [(j, fc) for j in range(NTL) for fc in range(32)]

            def load_w(idx):
                j, fc = seq[idx]
                w_ = wup[idx % 4]
                k.dma(w_.ap, WUPS.ap[fc], [WUPS], [w_], w_)

            def load_u(j):
                u_ = u2t[j % 2]
                k.dma(u_.ap, U2T.ap[:, :, j * 256:j * 256 + 258].rearrange("kc p t -> p kc t"), [U2T], [u_], u_)

            load_u(0)
            for i in range(3):
                load_w(i)
            for idx, (j, fc) in enumerate(seq):
                if fc == 0 and j + 1 < NTL:
                    load_u(j + 1)
                if idx + 3 < len(seq):
                    load_w(idx + 3)
                u_ = u2t[j % 2]
                w_ = wup[idx % 4]
                hh = hT[j % 2]
                bG, bU = bank[(idx % 2) * 2], bank[(idx % 2) * 2 + 1]
                for g, bb in enumerate((bG, bU)):
                    for kc in range(8):
                        k.mm(bb.ap[:, 0:258], w_.ap[:, g, kc, :], u_.ap[:, kc, :], kc == 0, kc == 7, [w_, u_], [bb], inc=(kc == 7))
                e = ctr["e"] % 3
                ctr["e"] += 1
                for bb, dst, ch in ((bG, cg[e], fc), (bU, cu[e], 32 + fc)):
                    k.act(dst.ap, bb.ap[:, 1:257], AF.Identity, [bb, cv], [dst], scale=cv.ap[:, ch, 1:2], bias=cv.ap[:, ch, 3:4])
                    k.stt(dst.ap, bb.ap[:, 0:256], cv.ap[:, ch, 0:1], dst.ap, ALU.mult, ALU.add, [bb, cv, dst], [dst])
                    k.stt(dst.ap, bb.ap[:, 2:258], cv.ap[:, ch, 2:3], dst.ap, ALU.mult, ALU.add, [bb, cv, dst], [dst])
                k.act(gl[e].ap, cg[e].ap, AF.Gelu_apprx_tanh, [cg[e]], [gl[e]])
                k.tt("dve", hh.ap[:, fc, :], gl[e].ap, cu[e].ap, ALU.mult, [gl[e], cu[e]], [hh])
                if fc == 31:
                    for tbb in range(2):
                        tb = j * 2 + tbb
                        rows = slice(tb * 128, (tb + 1) * 128)
                        ys = ctr["y"] % 2
                        ctr["y"] += 1
                        y0, y1 = bank[4 + ys * 2], bank[5 + ys * 2]
                        k.dma(h2r[ys].ap, H2.ap[rows, :], [H2], [h2r[ys]], h2r[ys])
                        for f2 in range(32):
                            for hf, yb in enumerate((y0, y1)):
                                k.mm(yb.ap, hh.ap[:, f2, tbb * 128:(tbb + 1) * 128], wdn.ap[:, f2, hf * 512:(hf + 1) * 512],
                                     f2 == 0, f2 == 31, [hh, wdn], [yb], inc=(f2 == 31))
                        k.act(junk.ap[:, 0:512], y0.ap, AF.Square, [y0], [junk, ssA], accum_out=ssA.ap[:, 0:1])
                        k.act(junk.ap[:, 512:1024], y1.ap, AF.Square, [y1, ssA], [junk, ssA], accum_out=ssA.ap[:, 1:2])
                        k.tt("dve", ss.ap, ssA.ap[:, 0:1], ssA.ap[:, 1:2], ALU.add, [ssA], [ss])
                        k.rstd(ss.ap, lnv, rs.ap, 1024, [ss], [rs])
                        o_ = h3[ys]
                        k.stt(o_.ap[:, 0:512], y0.ap, rs.ap, gpf[:, 0:512], ALU.mult, ALU.mult, [y0, rs, grow_t], [o_])
                        k.stt(o_.ap[:, 512:1024], y1.ap, rs.ap, gpf[:, 512:1024], ALU.mult, ALU.mult, [y1, rs, grow_t, o_], [o_])
                        k.tt("dve", o_.ap, o_.ap, h2r[ys].ap, ALU.add, [o_, h2r[ys]], [o_])
                        k.dma(H3.ap[rows, :], o_.ap, [o_], [H3], o_)

        def phase_fin():
            wg = k.sb([128, 8, 1024], BF16, "wg")
            k.dma(wg.ap, WG.ap.rearrange("(kc p) n -> p kc n", p=128), [WG], [wg], wg)
            wp = k.sb([128, 2, 1024], BF16, "wp")
            k.dma(wp.ap, WPLE.ap.rearrange("(kc p) n -> p kc n", p=128), [WPLE], [wp], wp)
            h3 = [k.sb([128, 1024], F32, "fh3_%d" % i) for i in range(3)]
            pp = [k.sb([128, 256], F32, "pp%d" % i) for i in range(3)]
            hb = [k.sb([128, 1024], BF16, "hb%d" % i) for i in range(2)]
            pb = [k.sb([128, 256], BF16, "pb%d" % i) for i in range(2)]
            h3T = [k.sb([128, 8, 128], BF16, "h3T%d" % i) for i in range(2)]
            pT = [k.sb([128, 2, 128], BF16, "pT%d" % i) for i in range(2)]
            den = k.sb([128, 1024], F32, "den")
            ev = k.sb([128, 1024], F32, "ev")
            ost = [k.sb([128, 1024], F32, "ost%d" % i) for i in range(2)]
            junk = k.sb([128, 1024], BF16, "njunk")
            ssA = k.sb([128, 2], F32, "nssA")
            ss = k.sb([128, 1], F32, "nss")
            lnv = k.sb([128, 1], F32, "nlnv")
            rs = k.sb([128, 1], F32, "nrs")
            gpl = grow_t.ap[:, 2048:3072]
            bT = T(k.bank_bf(7), bank[7].b)
            bP = T(k.bank_bf(6)[:, 0:2, :], bank[6].b)
            g0, g1, e0, e1 = bank[0], bank[1], bank[2], bank[3]

            def load(tb):
                s = tb % 3
                rows = slice(tb * 128, (tb + 1) * 128)
                k.dma(h3[s].ap, H3.ap[rows, :], [H3], [h3[s]], h3[s])
                k.dma(pp[s].ap, p_in[rows, :], [], [pp[s]], pp[s])

            def stage_a(tb):
                s, d = tb % 3, tb % 2
                k.copy("dve", hb[d].ap, h3[s].ap, [h3[s]], [hb[d]])
                k.copy("dve", pb[d].ap, pp[s].ap, [pp[s]], [pb[d]])
                for kc in range(8):
                    k.tr(bT.ap[:, kc, :], hb[d].ap[:, kc * 128:(kc + 1) * 128], ident.ap, [hb[d], ident], [bT], inc=(kc == 7))
                for kc in range(2):
                    k.tr(bP.ap[:, kc, :], pb[d].ap[:, kc * 128:(kc + 1) * 128], ident.ap, [pb[d], ident], [bP], inc=(kc == 1))
                k.copy("act", h3T[d].ap, bT.ap, [bT], [h3T[d]])
                k.copy("act", pT[d].ap, bP.ap, [bP], [pT[d]])

            def stage_mm(tb):
                d = tb % 2
                for hf, gb in enumerate((g0, g1)):
                    for kc in range(8):
                        k.mm(gb.ap, h3T[d].ap[:, kc, :], wg.ap[:, kc, hf * 512:(hf + 1) * 512], kc == 0, kc == 7, [h3T[d], wg], [gb], inc=(kc == 7))
                for hf, eb in enumerate((e0, e1)):
                    for kc in range(2):
                        k.mm(eb.ap, pT[d].ap[:, kc, :], wp.ap[:, kc, hf * 512:(hf + 1) * 512], kc == 0, kc == 1, [pT[d], wp], [eb], inc=(kc == 1))

            def stage_b(tb):
                s = tb % 3
                rows = slice(tb * 128, (tb + 1) * 128)
                k.act(den.ap[:, 0:512], g0.ap, AF.Exp, [g0], [den], scale=-1.0)
                k.act(den.ap[:, 512:1024], g1.ap, AF.Exp, [g1, den], [den], scale=-1.0)
                k.act(junk.ap[:, 0:512], e0.ap, AF.Square, [e0], [junk, ssA], accum_out=ssA.ap[:, 0:1])
                k.act(junk.ap[:, 512:1024], e1.ap, AF.Square, [e1, ssA], [junk, ssA], accum_out=ssA.ap[:, 1:2])
                k.tt("dve", ss.ap, ssA.ap[:, 0:1], ssA.ap[:, 1:2], ALU.add, [ssA], [ss])
                k.rstd(ss.ap, lnv, rs.ap, 1024, [ss], [rs])
                k.stt(ev.ap[:, 0:512], e0.ap, rs.ap, gpl[:, 0:512], ALU.mult, ALU.mult, [e0, rs, grow_t], [ev])
                k.stt(ev.ap[:, 512:1024], e1.ap, rs.ap, gpl[:, 512:1024], ALU.mult, ALU.mult, [e1, rs, grow_t, ev], [ev])
                k.act(den.ap, den.ap, AF.Ln, [den], [den], bias=1.0)
                k.act(den.ap, den.ap, AF.Exp, [den], [den], scale=-1.0)
                k.tt("dve", ev.ap, ev.ap, den.ap, ALU.mult, [ev, den], [ev])
                o_ = ost[tb % 2]
                k.tt("dve", o_.ap, ev.ap, h3[s].ap, ALU.add, [ev, h3[s]], [o_])
                k.dma(out_d[rows, :], o_.ap, [o_], [], o_)

            load(0)
            if NO > 1:
                load(1)
            stage_a(0)
            stage_mm(0)
            for tb in range(NO):
                if tb + 2 < NO:
                    load(tb + 2)
                if tb + 1 < NO:
                    stage_a(tb + 1)
                stage_b(tb)
                if tb + 1 < NO:
                    stage_mm(tb + 1)

        mj = phase_p0()
        phase_pa(mj)
        phases = [phase_att, phase_hg, phase_out, phase_ffn, phase_fin]
        for i, ph in enumerate(phases[:max(0, NPH - 2)]):
            new_phase()
            ph()
        k.barrier()
        k.emit_all()
    return nc


def _t5_bucket(rel):
    half, max_exact = 16, 8
    n = np.abs(rel)
    scaled = np.log(np.maximum(n, 1).astype(np.float32) / np.float32(max_exact)) / np.float32(math.log(128 / max_exact))
    large = np.minimum(max_exact + (scaled * np.float32(half - max_exact)).astype(np.int32), half - 1)
    return np.where(rel > 0, half, 0) + np.where(n < max_exact, n, large)


def _consts():
    c = np.zeros((128, 904), np.float32)
    s = np.arange(128)[:, None]
    t = np.arange(128)[None, :]
    same = (s // 64) == (t // 64)
    c[:, 0:128] = np.eye(128, dtype=np.float32)
    Uf = (same & (s <= t)).astype(np.float32)
    Ub = (same & (s >= t)).astype(np.float32)
    J = same.astype(np.float32)
    c[:, 128:256] = Uf
    c[:, 256:384] = Ub
    c[:, 384:512] = J - Uf
    c[:, 512:640] = J - Ub
    c[:, 896] = (np.arange(128) < 64)
    c[:, 897] = (np.arange(128) >= 64)
    return c


def make_in_maps(inputs, n_seq, NB, NO):
    f = lambda a: np.ascontiguousarray(a, dtype=np.float32)
    x, p = np.asarray(inputs["x"]), np.asarray(inputs["p"])
    rel_bias = np.asarray(inputs["rel_bias"])
    w_in = np.asarray(inputs["w_in"])[0]
    lb_param = np.asarray(inputs["lb_param"])[:, 0:2, :]
    conv_w = np.asarray(inputs["conv_w"])[0]
    conv_b = np.asarray(inputs["conv_b"])[0]
    bc = lambda v: np.broadcast_to(np.asarray(v, np.float32).reshape(1, -1), (128, np.asarray(v).size))
    grow = np.concatenate([bc(inputs["g_post_mix"][0]), bc(inputs["g_post_ffn"][0]), bc(inputs["g_ple"][0]),
                           bc(inputs["g_diff"][0]), bc(inputs["g_hgrn"][0])], axis=1)
    gcols = np.concatenate([np.asarray(inputs["g_pre_mix"][0]).reshape(8, 128).T,
                            np.asarray(inputs["g_pre_ffn"][0]).reshape(8, 128).T], axis=1)
    lamv = np.concatenate([bc(inputs["lambda_q1"][0]), bc(inputs["lambda_k1"][0]), bc(inputs["lambda_q2"][0]),
                           bc(inputs["lambda_k2"][0])], axis=1)
    cst = _consts()
    kk = np.arange(128)[:, None, None]
    dd = np.arange(9)[None, :, None] - 4
    qq = np.arange(128)[None, None, :]
    rel = dd * 128 + kk - qq
    shared = {
        "w_out": f(inputs["w_out"][0]), "w_up": f(inputs["w_up"][0]), "w_down": f(inputs["w_down"][0]),
        "w_ple": f(inputs["w_ple"][0]), "w_gate": f(inputs["w_ple_gate"][0]),
        "gcols": f(gcols), "grow": f(grow), "lamv": f(lamv), "cst": f(cst),
    }
    maps = []
    for c in range(2 * n_seq):
        b, rev = c // 2, c % 2
        xs, ps = x[b], p[0, b]
        wi, lbq, cw = w_in, lb_param, conv_w
        rl = rel
        if rev:
            xs, ps = xs[::-1], ps[::-1]
            wi = np.concatenate([w_in[:, :2560], w_in[:, 3072:3584], w_in[:, 2560:3072], w_in[:, 3584:]], axis=1)
            lbq = lb_param[::-1]
            cw = conv_w[::-1]
            rl = -rel
        bstrip = rel_bias[_t5_bucket(rl.astype(np.int32))]
        bstrip = np.transpose(bstrip, (3, 0, 1, 2)).reshape(4, 128, 9 * 128)
        convc = np.concatenate([cw.reshape(3, 64, 128).transpose(2, 1, 0), conv_b.reshape(64, 128).T[:, :, None]], axis=2)
        m = dict(shared)
        m.update({
            "x": f(xs[:NB * 128]), "p": f(ps[:NO * 128]), "w_in": f(wi),
            "lbp": f(np.broadcast_to(lbq.reshape(1, -1), (128, 2048))),
            "convc": f(convc.reshape(128, 256)), "bstrip": f(bstrip),
        })
        maps.append(m)
    return maps


_CACHE = {}
_DBG = ()
_NPH = 7
_LAST = {}


def kernel(**inputs):
    inputs = {k_: np.asarray(v) for k_, v in inputs.items()}
    x = inputs["x"]
    n_seq, tlen = x.shape[0], x.shape[1]
    NB = tlen // 128
    NO = NB // 2
    key = (NB, NO)
    if key not in _CACHE:
        _CACHE[key] = build(NB, NO, dbg=_DBG, NPH=_NPH)
    nc = _CACHE[key]
    maps = make_in_maps(inputs, n_seq, NB, NO)
    res = run_bass_kernel_spmd(nc, maps, core_ids=list(range(2 * n_seq)))
    _LAST["res"] = res
    out = np.empty((n_seq, tlen, D), np.float32)
    half = NO * 128
    for c in range(2 * n_seq):
        o = np.asarray(res.results[c]["out"])
        if c % 2 == 0:
            out[c // 2, :half] = o
        else:
            out[c // 2, half:] = o[::-1]
    return out
```
